# Optimizing a Trainium2 kernel written in Bass

```python
import jax
import jax.numpy as jnp
from jax import lax
import numpy as np

D_MODEL = 2048
BATCH = 2
SEQ = 8192
DEPTH = 4

GRID_W = 64
CTX_LEN = 256
EPS = 1e-6

NA_HEADS = 16
NA_HEAD_DIM = 128
NA_WIDTH = NA_HEADS * NA_HEAD_DIM
NA_KH = 8
NA_KW = 16

SSD_HEAD_DIM = 64
SSD_WIDTH = D_MODEL
SSD_HEADS = SSD_WIDTH // SSD_HEAD_DIM
SSD_GROUPS = 8
SSD_HPG = SSD_HEADS // SSD_GROUPS
SSD_STATE = 128
SSD_CONV = 5
SSD_CHUNK = 128
SSD_XBC = SSD_WIDTH + 2 * SSD_GROUPS * SSD_STATE

SC_WIDTH = D_MODEL
SC_CONV = 3

MIX_WIDTH = NA_WIDTH + SSD_WIDTH
COL_Q = 0
COL_GATE_A = COL_Q + NA_WIDTH
COL_Z = COL_GATE_A + NA_WIDTH
COL_K = COL_Z + SSD_WIDTH
COL_V = COL_K + NA_WIDTH
COL_XBC = COL_V + NA_WIDTH
COL_DT = COL_XBC + SSD_XBC
EVEN_IN = COL_DT + 2 * SSD_HEADS
ODD_IN = 4 * SC_WIDTH
N_EVEN = (DEPTH + 1) // 2
N_ODD = DEPTH // 2

kernel_name = 'hybrid_natten_ssd_shortconv_dit'


def rms_norm(x, g):
    xf = x.astype(jnp.float32)
    y = xf * lax.rsqrt(jnp.mean(xf * xf, axis=-1, keepdims=True) + EPS)
    return (y * g.astype(jnp.float32)).astype(x.dtype)


def adaln(cond, w, b):
    m = jax.nn.silu(cond) @ w + b
    return jnp.split(m, 3, axis=-1)


def modulate(h, shift, scale):
    return h * (1 + scale) + shift


def cols(p, start, width, base):
    return p[..., start - base:start - base + width]


def heads(t, n, d):
    return t.reshape(*t.shape[:2], n, d)


def dw_conv(x, w, b=None):
    k = w.shape[0]
    y = lax.conv_general_dilated(x, w[:, None, :], window_strides=(1,), padding=[(k // 2, k // 2)],
                                 dimension_numbers=('NWC', 'WIO', 'NWC'), feature_group_count=x.shape[-1])
    return y if b is None else y + b


def neighbourhood_attention(q, k, v, k_ctx, v_ctx, rpb):
    b, seq, nh, dh = q.shape
    rows = seq // GRID_W
    kh = min(NA_KH, rows)
    scale = dh ** -0.5
    qg = q.reshape(b, rows, GRID_W, nh, dh)
    kg = k.reshape(b, rows, GRID_W, nh, dh)
    vg = v.reshape(b, rows, GRID_W, nh, dh)
    col = jnp.arange(GRID_W)
    col_start = jnp.clip(col - NA_KW // 2, 0, GRID_W - NA_KW)
    in_win = (col[None, :] >= col_start[:, None]) & (col[None, :] < col_start[:, None] + NA_KW)
    mask = jnp.broadcast_to(in_win[:, None, :], (GRID_W, kh, GRID_W)).reshape(GRID_W, kh * GRID_W)
    dc_idx = jnp.clip(col[None, :] - col[:, None], -(NA_KW - 1), NA_KW - 1) + NA_KW - 1

    def row_block(r):
        rs = jnp.clip(r - kh // 2, 0, rows - kh)
        qr = lax.dynamic_index_in_dim(qg, r, axis=1, keepdims=False)
        kb = lax.dynamic_slice_in_dim(kg, rs, kh, axis=1).reshape(b, kh * GRID_W, nh, dh)
        vb = lax.dynamic_slice_in_dim(vg, rs, kh, axis=1).reshape(b, kh * GRID_W, nh, dh)
        dr_idx = rs + jnp.arange(kh) - r + NA_KH - 1
        bias = rpb[:, dr_idx[None, :, None], dc_idx[:, None, :]].reshape(nh, GRID_W, kh * GRID_W)
        s_lat = jnp.einsum('bqhd,bkhd->bhqk', qr, kb).astype(jnp.float32) * scale + bias.astype(jnp.float32)
        s_lat = jnp.where(mask, s_lat, -jnp.inf)
        s_ctx = jnp.einsum('bqhd,bkhd->bhqk', qr, k_ctx).astype(jnp.float32) * scale
        pr = jax.nn.softmax(jnp.concatenate([s_lat, s_ctx], axis=-1), axis=-1).astype(v.dtype)
        return (jnp.einsum('bhqk,bkhd->bqhd', pr[..., :kh * GRID_W], vb)
                + jnp.einsum('bhqk,bkhd->bqhd', pr[..., kh * GRID_W:], v_ctx))

    out = lax.map(row_block, jnp.arange(rows))
    return jnp.moveaxis(out, 0, 1).reshape(b, seq, nh, dh)


def context_attention(q, k, v):
    s = jnp.einsum('bqhd,bkhd->bhqk', q, k).astype(jnp.float32) * q.shape[-1] ** -0.5
    pr = jax.nn.softmax(s, axis=-1).astype(v.dtype)
    return jnp.einsum('bhqk,bkhd->bqhd', pr, v)


def ssd_scan(x, dt, a, bm, cm, h0):
    b, seq = x.shape[:2]
    nc = seq // SSD_CHUNK

    def chunks(t):
        return jnp.moveaxis(t.reshape(b, nc, SSD_CHUNK, *t.shape[2:]), 1, 0)

    tri = jnp.tril(jnp.ones((SSD_CHUNK, SSD_CHUNK), bool))[None, :, :, None, None]

    def step(h, inp):
        xc, dtc, bc, cc = inp
        cs = jnp.cumsum(dtc * a, axis=1)
        seg = cs[:, :, None] - cs[:, None, :]
        decay = jnp.exp(jnp.where(tri, seg, -jnp.inf))
        cb = jnp.einsum('bign,bjgn->bijg', cc, bc)
        w = cb[..., None] * decay * dtc[:, None]
        y = jnp.einsum('bijgr,bjgrp->bigrp', w, xc)
        y = y + jnp.einsum('bign,bgrpn->bigrp', cc, h) * jnp.exp(cs)[..., None]
        to_end = jnp.exp(cs[:, -1:] - cs) * dtc
        h = h * jnp.exp(cs[:, -1])[..., None, None] + jnp.einsum('bjgn,bjgr,bjgrp->bgrpn', bc, to_end, xc)
        return h, y

    h, ys = lax.scan(step, h0, (chunks(x), chunks(dt), chunks(bm), chunks(cm)))
    return jnp.moveaxis(ys, 0, 1).reshape(x.shape), h


def ssd_final_state(x, dt, a, bm):
    cs = jnp.cumsum(dt * a, axis=1)
    to_end = jnp.exp(cs[:, -1:] - cs) * dt
    return jnp.einsum('bjgn,bjgr,bjgrp->bgrpn', bm, to_end, x)


def _flip(t, rev):
    return jnp.flip(t, axis=1) if rev else t


def bidirectional_ssd(lat, con, a, want_ctx):
    xs, dt, bm, cm = lat
    xs_c, dt_c, bm_c, cm_c = con
    b = xs.shape[0]
    y = jnp.zeros_like(xs)
    yc = jnp.zeros_like(xs_c) if want_ctx else None
    for d in range(2):
        rev = d == 1
        if want_ctx:
            h0 = jnp.zeros((b, SSD_GROUPS, SSD_HPG, SSD_HEAD_DIM, SSD_STATE), jnp.float32)
            yc_d, hc = ssd_scan(_flip(xs_c, rev), _flip(dt_c[:, :, d], rev), a[d], _flip(bm_c, rev), _flip(cm_c, rev), h0)
            yc = yc + _flip(yc_d, rev)
        else:
            hc = ssd_final_state(_flip(xs_c, rev), _flip(dt_c[:, :, d], rev), a[d], _flip(bm_c, rev))
        y_d, _ = ssd_scan(_flip(xs, rev), _flip(dt[:, :, d], rev), a[d], _flip(bm, rev), _flip(cm, rev), hc)
        y = y + _flip(y_d, rev)
    return y, yc


def ssd_inputs(p, base, conv_w, conv_b, dt_bias):
    b, seq = p.shape[:2]
    gn = SSD_GROUPS * SSD_STATE
    xbc = jax.nn.silu(dw_conv(cols(p, COL_XBC, SSD_XBC, base), conv_w, conv_b)).astype(jnp.float32)
    xs = xbc[..., :SSD_WIDTH].reshape(b, seq, SSD_GROUPS, SSD_HPG, SSD_HEAD_DIM)
    bm = xbc[..., SSD_WIDTH:SSD_WIDTH + gn].reshape(b, seq, SSD_GROUPS, SSD_STATE)
    cm = xbc[..., SSD_WIDTH + gn:].reshape(b, seq, SSD_GROUPS, SSD_STATE)
    dt_raw = cols(p, COL_DT, 2 * SSD_HEADS, base).astype(jnp.float32).reshape(b, seq, 2, SSD_GROUPS, SSD_HPG)
    dt = jax.nn.softplus(dt_raw + dt_bias.astype(jnp.float32).reshape(2, SSD_GROUPS, SSD_HPG))
    return xs, dt, bm, cm


def gated_group_rmsnorm(y, z, g):
    b, seq = y.shape[:2]
    yz = (y.reshape(b, seq, SSD_WIDTH) * jax.nn.silu(z.astype(jnp.float32))).reshape(b, seq, SSD_GROUPS, -1)
    yz = yz * lax.rsqrt(jnp.mean(yz * yz, axis=-1, keepdims=True) + EPS)
    return (yz.reshape(b, seq, SSD_WIDTH) * g.astype(jnp.float32)).astype(z.dtype)


def na_ssd_mixer(h, hc, w_in, conv_w, conv_b, a_log, dt_bias, d_skip, ssm_norm_g,
                 q_norm_g, k_norm_g, rpb, w_out, update_ctx):
    b, seq, _ = h.shape
    p = h @ w_in
    base = 0 if update_ctx else COL_K
    pc = hc @ w_in[:, base:]
    q = rms_norm(heads(cols(p, COL_Q, NA_WIDTH, 0), NA_HEADS, NA_HEAD_DIM), q_norm_g)
    k = rms_norm(heads(cols(p, COL_K, NA_WIDTH, 0), NA_HEADS, NA_HEAD_DIM), k_norm_g)
    v = heads(cols(p, COL_V, NA_WIDTH, 0), NA_HEADS, NA_HEAD_DIM)
    kc = rms_norm(heads(cols(pc, COL_K, NA_WIDTH, base), NA_HEADS, NA_HEAD_DIM), k_norm_g)
    vc = heads(cols(pc, COL_V, NA_WIDTH, base), NA_HEADS, NA_HEAD_DIM)
    ya = neighbourhood_attention(q, k, v, kc, vc, rpb).reshape(b, seq, NA_WIDTH)
    ya = ya * jax.nn.silu(cols(p, COL_GATE_A, NA_WIDTH, 0))
    a = -jnp.exp(a_log.astype(jnp.float32)).reshape(2, SSD_GROUPS, SSD_HPG)
    d = d_skip.astype(jnp.float32).reshape(SSD_GROUPS, SSD_HPG)[..., None]
    lat = ssd_inputs(p, 0, conv_w, conv_b, dt_bias)
    con = ssd_inputs(pc, base, conv_w, conv_b, dt_bias)
    y_ssd, yc_ssd = bidirectional_ssd(lat, con, a, update_ctx)
    yb = gated_group_rmsnorm(y_ssd + d * lat[0], cols(p, COL_Z, SSD_WIDTH, 0), ssm_norm_g)
    out = jnp.concatenate([ya, yb], axis=-1) @ w_out
    out_c = None
    if update_ctx:
        n_ctx = hc.shape[1]
        qc = rms_norm(heads(cols(pc, COL_Q, NA_WIDTH, 0), NA_HEADS, NA_HEAD_DIM), q_norm_g)
        yac = context_attention(qc, kc, vc).reshape(b, n_ctx, NA_WIDTH) * jax.nn.silu(cols(pc, COL_GATE_A, NA_WIDTH, 0))
        ybc = gated_group_rmsnorm(yc_ssd + d * con[0], cols(pc, COL_Z, SSD_WIDTH, 0), ssm_norm_g)
        out_c = jnp.concatenate([yac, ybc], axis=-1) @ w_out
    return out, out_c


def short_conv_mixer(h, w_in, conv_w, w_out):
    p = h @ w_in
    bg, cg, hv, g = jnp.split(p, 4, axis=-1)
    y = bg * dw_conv(cg * hv, conv_w)
    return (jax.nn.silu(g) * y) @ w_out


def setup_inputs(seed: int = 0) -> dict:
    key = jax.random.key(seed)
    ks = jax.random.split(key, 21)
    nrm = jax.random.normal
    dt0 = jnp.exp(jax.random.uniform(ks[11], (N_EVEN, 2, SSD_HEADS), minval=float(np.log(1e-3)), maxval=float(np.log(1e-1))))
    return {
        'x': nrm(ks[0], (BATCH, SEQ, D_MODEL), jnp.float32),
        'c': nrm(ks[1], (BATCH, D_MODEL), jnp.float32),
        'ctx': nrm(ks[2], (BATCH, CTX_LEN, D_MODEL), jnp.float32),
        'c_ctx': nrm(ks[3], (D_MODEL,), jnp.float32),
        'ada_w': nrm(ks[4], (DEPTH, D_MODEL, 3 * D_MODEL), jnp.float32) * (0.5 * D_MODEL ** -0.5),
        'ada_b': nrm(ks[5], (DEPTH, 3 * D_MODEL), jnp.float32) * 0.02,
        'norm_g': 1.0 + 0.05 * nrm(ks[6], (DEPTH, D_MODEL), jnp.float32),
        'na_ssd_w_in': nrm(ks[7], (N_EVEN, D_MODEL, EVEN_IN), jnp.float32) * D_MODEL ** -0.5,
        'ssd_conv_w': nrm(ks[8], (N_EVEN, SSD_CONV, SSD_XBC), jnp.float32) * SSD_CONV ** -0.5,
        'ssd_conv_b': nrm(ks[9], (N_EVEN, SSD_XBC), jnp.float32) * 0.02,
        'ssd_a_log': jnp.log(jax.random.uniform(ks[10], (N_EVEN, 2, SSD_HEADS), minval=1.0, maxval=16.0)),
        'ssd_dt_bias': dt0 + jnp.log(-jnp.expm1(-dt0)),
        'ssd_d': 1.0 + 0.1 * nrm(ks[12], (N_EVEN, SSD_HEADS), jnp.float32),
        'ssd_norm_g': 1.0 + 0.05 * nrm(ks[13], (N_EVEN, SSD_WIDTH), jnp.float32),
        'q_norm_g': 1.0 + 0.05 * nrm(ks[14], (N_EVEN, NA_HEAD_DIM), jnp.float32),
        'k_norm_g': 1.0 + 0.05 * nrm(ks[15], (N_EVEN, NA_HEAD_DIM), jnp.float32),
        'na_rpb': nrm(ks[16], (N_EVEN, NA_HEADS, 2 * NA_KH - 1, 2 * NA_KW - 1), jnp.float32) * 0.1,
        'na_ssd_w_out': nrm(ks[17], (N_EVEN, MIX_WIDTH, D_MODEL), jnp.float32) * MIX_WIDTH ** -0.5,
        'sc_w_in': nrm(ks[18], (N_ODD, D_MODEL, ODD_IN), jnp.float32) * D_MODEL ** -0.5,
        'sc_conv_w': nrm(ks[19], (N_ODD, SC_CONV, SC_WIDTH), jnp.float32) * SC_CONV ** -0.5,
        'sc_w_out': nrm(ks[20], (N_ODD, SC_WIDTH, D_MODEL), jnp.float32) * SC_WIDTH ** -0.5,
    }


def reference(x, c, ctx, c_ctx, ada_w, ada_b, norm_g, na_ssd_w_in, ssd_conv_w, ssd_conv_b,
              ssd_a_log, ssd_dt_bias, ssd_d, ssd_norm_g, q_norm_g, k_norm_g, na_rpb,
              na_ssd_w_out, sc_w_in, sc_conv_w, sc_w_out):
    for i in range(DEPTH):
        update_ctx = any(j % 2 == 0 for j in range(i + 1, DEPTH))
        needs_ctx = (i % 2 == 0) or update_ctx
        shift, scale, gate = adaln(c, ada_w[i], ada_b[i])
        h = modulate(rms_norm(x, norm_g[i]), shift[:, None], scale[:, None])
        if needs_ctx:
            shift_c, scale_c, gate_c = adaln(c_ctx, ada_w[i], ada_b[i])
            hc = modulate(rms_norm(ctx, norm_g[i]), shift_c, scale_c)
        if i % 2 == 0:
            e = i // 2
            y, yc = na_ssd_mixer(h, hc, na_ssd_w_in[e], ssd_conv_w[e], ssd_conv_b[e], ssd_a_log[e],
                                 ssd_dt_bias[e], ssd_d[e], ssd_norm_g[e], q_norm_g[e], k_norm_g[e],
                                 na_rpb[e], na_ssd_w_out[e], update_ctx)
        else:
            o = i // 2
            y = short_conv_mixer(h, sc_w_in[o], sc_conv_w[o], sc_w_out[o])
            yc = short_conv_mixer(hc, sc_w_in[o], sc_conv_w[o], sc_w_out[o]) if update_ctx else None
        x = x + gate[:, None] * y
        if update_ctx:
            ctx = ctx + gate_c * yc
    return x
```

```python
import contextlib
import numpy as np
import concourse.bass as bass
import concourse.mybir as mybir
from concourse.bass_utils import run_bass_kernel_spmd

F32 = mybir.dt.float32
BF16 = mybir.dt.bfloat16
AF = mybir.ActivationFunctionType
ALU = mybir.AluOpType

D_MODEL = 2048
BATCH = 2
SEQ = 8192
GRID_W = 64
CTX = 256
EPS = 1e-6
NSEG = 4
EVEN_IN = 14400
COL_Q, COL_GATE, COL_Z, COL_K, COL_V, COL_XBC, COL_DT = 0, 2048, 4096, 6144, 8192, 10240, 14336
NEG = -30000.0

COMPUTE = ("pe", "act", "dve", "pool")
NDMASEM = 12
ENGNAME = {"sp": "sync", "act": "scalar", "pool": "gpsimd", "dve": "vector", "pe": "tensor"}


class SemPool:
    def __init__(self, nc):
        self.nc = nc
        self.es = contextlib.ExitStack()
        self.csem = {e: self.es.enter_context(nc.semaphore("c_" + e)) for e in COMPUTE}
        self.dsem = {q: [self.es.enter_context(nc.semaphore("d_%s_%d" % (q, i))) for i in range(NDMASEM)]
                     for q in ("sp", "act")}
        self.cbase = {e: 0 for e in COMPUTE}
        self.dcount = {"sp": 0, "act": 0}

    def close(self):
        self.es.close()


def sempool(nc):
    if not hasattr(nc, "_sempool"):
        nc._sempool = SemPool(nc)
    return nc._sempool


class Prog:
    def __init__(self, nc):
        self.nc = nc
        self.sp = sempool(nc)
        self.ops = []
        self.lastw = {}
        self.readers = {}
        self.dma_count = dict(self.sp.dcount)

    def _add(self, eng, kind, fn, reads, writes):
        idx = len(self.ops)
        deps = set()
        for k in reads:
            w = self.lastw.get(k)
            if w is not None:
                deps.add(w)
        for k in writes:
            w = self.lastw.get(k)
            if w is not None:
                deps.add(w)
            for r in self.readers.get(k, ()):
                deps.add(r)
        op = dict(eng=eng, kind=kind, fn=fn, deps=deps, idx=idx, sig=False)
        if kind == "d":
            op["dn"] = self.dma_count[eng]
            self.dma_count[eng] += 1
        self.ops.append(op)
        for k in reads:
            self.readers.setdefault(k, []).append(idx)
        for k in writes:
            self.lastw[k] = idx
            self.readers[k] = []
        return idx

    def op(self, eng, fn, reads=(), writes=()):
        return self._add(eng, "c", fn, tuple(reads), tuple(writes))

    def dma(self, eng, out, in_, reads=(), writes=(), **kw):
        return self._add(eng, "d", lambda e: e.dma_start(out=out, in_=in_, **kw), tuple(reads), tuple(writes))

    def emit(self):
        nc, ops, sp = self.nc, self.ops, self.sp
        csem, dsem = sp.csem, sp.dsem
        streams = {e: [] for e in ("pe", "act", "dve", "pool", "sp")}
        for o in ops:
            streams[o["eng"]].append(o)

        def skip(t, o):
            return t["kind"] == "c" and o["kind"] == "c" and t["eng"] == "pe" and o["eng"] == "pe"

        for o in ops:
            for d in o["deps"]:
                t = ops[d]
                if t["kind"] == "c" and not skip(t, o):
                    t["sig"] = True
        for e in COMPUTE:
            c = sp.cbase[e]
            for o in streams[e]:
                if o["kind"] == "c" and o["sig"]:
                    c += 1
                    o["sigval"] = c
            sp.cbase[e] = c
        for e, st in streams.items():
            waited = {}
            for o in st:
                waits = {}
                for d in o["deps"]:
                    t = ops[d]
                    if t["kind"] == "c":
                        if skip(t, o):
                            continue
                        key, val = ("c", t["eng"]), t["sigval"]
                    else:
                        key, val = ("d", t["eng"], t["dn"] % NDMASEM), 16 * (t["dn"] // NDMASEM + 1)
                    if val > waits.get(key, 0):
                        waits[key] = val
                if o["kind"] == "d" and o["dn"] >= NDMASEM:
                    key, val = ("d", e, o["dn"] % NDMASEM), 16 * (o["dn"] // NDMASEM)
                    if val > waits.get(key, 0):
                        waits[key] = val
                fw = []
                for key, val in waits.items():
                    if waited.get(key, 0) >= val:
                        continue
                    waited[key] = val
                    fw.append((key, val))
                o["waits"] = fw
        with nc.Block() as block:
            def make(e):
                def body(eng):
                    for o in streams[e]:
                        for key, val in o["waits"]:
                            sem = csem[key[1]] if key[0] == "c" else dsem[key[1]][key[2]]
                            eng.wait_ge(sem, val)
                        ins = o["fn"](eng)
                        if o["kind"] == "d":
                            ins.then_inc(dsem[e][o["dn"] % NDMASEM], 16)
                        elif o["sig"]:
                            ins.then_inc(csem[e], 1)
                    if e in ("sp", "act"):
                        n = self.dma_count[e]
                        for s in range(min(n, NDMASEM)):
                            eng.wait_ge(dsem[e][s], 16 * ((n - s + NDMASEM - 1) // NDMASEM))
                return body
            for e in ("sp", "act", "pool", "dve", "pe"):
                if streams[e] or e in ("sp", "act"):
                    getattr(block, ENGNAME[e])(make(e))
        sp.dcount = dict(self.dma_count)


class Phase:
    _cnt = 0

    def __init__(self, nc, name):
        self.nc, self.name = nc, name
        self.es = contextlib.ExitStack()
        self.P = Prog(nc)
        self.q = 0

    def __enter__(self):
        return self

    def __exit__(self, et, ev, tb):
        if et is None:
            self.P.emit()
        self.es.close()
        return False

    def _nm(self, name):
        Phase._cnt += 1
        return "%s_%s_%d" % (self.name, name, Phase._cnt)

    def sb(self, shape, dt, name="t"):
        return self.es.enter_context(self.nc.sbuf_tensor(self._nm(name), list(shape), dt))

    def ps(self, shape=(128, 512), dt=F32, name="p"):
        return self.es.enter_context(self.nc.psum_tensor(self._nm(name), list(shape), dt))

    def ldq(self):
        self.q ^= 1
        return "sp"


def phase_mod(nc, c2T, ada_w, ada_b, norm_g, modD, tag):
    with Phase(nc, "mod" + tag) as ph:
        P = ph.P
        c2 = ph.sb([128, 16, 2], F32)
        sil = ph.sb([128, 16, 2], F32)
        mod = ph.sb([2, 6144], F32)
        bia = ph.sb([2, 6144], F32)
        gn = ph.sb([2, 2048], F32)
        wt = [ph.sb([128, 16, 512], F32, "wt") for _ in range(2)]
        pm = [ph.ps() for _ in range(2)]
        P.dma("sp", c2[:], c2T, writes=["c2"])
        P.dma("sp", bia[:], ada_b.partition_broadcast(2), writes=["bia"])
        P.dma("sp", gn[:], norm_g.partition_broadcast(2), writes=["gn"])
        P.op("act", lambda e: e.activation(sil[:], c2[:], AF.Silu), reads=["c2"], writes=["sil"])
        for cb in range(12):
            b = cb % 2
            src = ada_w[:, cb * 512:(cb + 1) * 512].rearrange("(c p) n -> p c n", p=128)
            for s in range(4):
                P.dma("sp", wt[b][:, 4 * s:4 * s + 4, :], src[:, 4 * s:4 * s + 4, :], writes=[("wt", b, s)])
            for kc in range(16):
                P.op("pe", lambda e, b=b, kc=kc: e.matmul(pm[b][0:2, :], sil[:, kc, :], wt[b][:, kc, :],
                                                          start=(kc == 0), stop=(kc == 15)),
                     reads=["sil", ("wt", b, kc // 4)], writes=[("pm", b)])
            P.op("dve", lambda e, b=b, cb=cb: e.tensor_tensor(mod[:, cb * 512:(cb + 1) * 512], pm[b][0:2, :],
                                                              bia[:, cb * 512:(cb + 1) * 512], ALU.add),
                 reads=[("pm", b), "bia"], writes=["mod"])
        P.op("dve", lambda e: e.scalar_tensor_tensor(mod[:, 2048:4096], mod[:, 2048:4096], 1.0, gn[:], ALU.add, ALU.mult),
             reads=["mod", "gn"], writes=["mod"])
        P.dma("sp", modD, mod[:], reads=["mod"], writes=["modD"])


def phase_norm(nc, segs, modD, HT, ident_d, tag):
    with Phase(nc, "nrm" + tag) as ph:
        P = ph.P
        rows = sorted(set(r for _, r, _ in segs))
        G = {r: ph.sb([128, 2048], F32, "G") for r in rows}
        S = {r: ph.sb([128, 2048], F32, "S") for r in rows}
        ident = ph.sb([128, 128], BF16, "id")
        P.dma("sp", ident[:], ident_d, writes=["ident"])
        for r in rows:
            P.dma("sp", G[r][:], modD[r, 2048:4096].partition_broadcast(128), writes=[("G", r)])
            P.dma("sp", S[r][:], modD[r, 0:2048].partition_broadcast(128), writes=[("S", r)])
        xt = [ph.sb([128, 2048], F32, "x") for _ in range(2)]
        t1 = [ph.sb([128, 2048], F32, "t1") for _ in range(2)]
        hb = [ph.sb([128, 2048], BF16, "hb") for _ in range(2)]
        ho = [ph.sb([128, 16, 128], BF16, "ho") for _ in range(2)]
        st = [ph.sb([128, 4], F32, "st") for _ in range(2)]
        junk = [ph.sb([128, 2048], BF16, "junk") for _ in range(2)]
        pT = [ph.ps([128, 8, 128], BF16) for _ in range(2)]
        it = 0
        tok0 = 0
        for src, r, n in segs:
            for ti in range(n // 128):
                b = it % 2
                P.dma("sp", xt[b][:], src[ti * 128:(ti + 1) * 128, :], writes=[("x", b)])
                P.op("act", lambda e, b=b: e.activation(junk[b][:], xt[b][:], AF.Square, accum_out=st[b][:, 0:1]),
                     reads=[("x", b)], writes=[("junk", b), ("st0", b)])
                P.op("act", lambda e, b=b: e.activation(st[b][:, 1:2], st[b][:, 0:1], AF.Sqrt, scale=1.0 / D_MODEL, bias=EPS),
                     reads=[("st0", b)], writes=[("st1", b)])
                P.op("dve", lambda e, b=b: e.reciprocal(st[b][:, 2:3], st[b][:, 1:2]), reads=[("st1", b)], writes=[("st2", b)])
                P.op("dve", lambda e, b=b, r=r: e.scalar_tensor_tensor(t1[b][:], xt[b][:], st[b][:, 2:3], G[r][:], ALU.mult, ALU.mult),
                     reads=[("x", b), ("st2", b), ("G", r)], writes=[("t1", b)])
                P.op("pool", lambda e, b=b, r=r: e.tensor_tensor(hb[b][:], t1[b][:], S[r][:], ALU.add),
                     reads=[("t1", b), ("S", r)], writes=[("hb", b)])
                for half in range(2):
                    for j in range(8):
                        kc = half * 8 + j
                        P.op("pe", lambda e, b=b, kc=kc, half=half, j=j: e.transpose(pT[half][:, j, :], hb[b][:, kc * 128:(kc + 1) * 128], ident[:]),
                             reads=[("hb", b), "ident"], writes=[("pT", half)])
                    eng = "act" if half == 0 else "dve"
                    if eng == "act":
                        P.op("act", lambda e, b=b, half=half: e.copy(ho[b][:, half * 8:half * 8 + 8, :], pT[half][:]),
                             reads=[("pT", half)], writes=[("ho", b, half)])
                    else:
                        P.op("dve", lambda e, b=b, half=half: e.tensor_copy(ho[b][:, half * 8:half * 8 + 8, :], pT[half][:]),
                             reads=[("pT", half)], writes=[("ho", b, half)])
                t = tok0 + ti * 128
                P.dma("act", HT[:, t:t + 128].rearrange("(c p) n -> p c n", p=128), ho[b][:],
                      reads=[("ho", b, 0), ("ho", b, 1)], writes=["HT"])
                it += 1
            tok0 += n


def phase_proj(nc, HT, NT, fams, tag):
    with Phase(nc, "prj" + tag) as ph:
        P = ph.P
        hT = ph.sb([128, 16, NT], BF16, "hT")
        src = HT.rearrange("(c p) n -> p c n", p=128)
        for c in range(16):
            P.dma("sp", hT[:, c, :], src[:, c, :], writes=["hT"])
        wf = [ph.sb([128, 16, 256], F32, "wf") for _ in range(2)]
        wb = [ph.sb([128, 16, 256], BF16, "wb") for _ in range(2)]
        acc = [ph.ps() for _ in range(3)]
        nps = [ph.ps() for _ in range(2)]
        of = [ph.sb([128, 512], F32, "of") for _ in range(2)]
        ob = [ph.sb([128, 512], BF16, "ob") for _ in range(2)]
        sq = [ph.sb([128, 512], BF16, "sq") for _ in range(2)]
        rs = [ph.sb([128, 512], F32, "rs") for _ in range(2)]
        onesb = ph.sb([128, 128], BF16, "ones")
        P.op("pool", lambda e: e.memset(onesb[:], 1.0), writes=["ones"])
        gains = {}
        for fi, f in enumerate(fams):
            if f.get("norm") is not None:
                g = ph.sb([128, 1], F32, "gain")
                P.dma("sp", g[:], f["norm"], writes=[("gain", fi)])
                gains[fi] = g
        tblocks = [(t, min(512, NT - t)) for t in range(0, NT, 512)]
        wi = 0
        ai = 0
        oi = 0
        ni = 0
        for fi, f in enumerate(fams):
            W, col0, ncols, out, odt = f["W"], f["col0"], f["ncols"], f["out"], f["odt"]
            for g0 in range(0, ncols, 256):
                gw = min(256, ncols - g0)
                b = wi % 2
                wi += 1
                wsrc = W[:, col0 + g0:col0 + g0 + gw].rearrange("(c p) n -> p c n", p=128)
                for s in range(2):
                    P.dma("sp", wf[b][:, 8 * s:8 * s + 8, 0:gw], wsrc[:, 8 * s:8 * s + 8, :], writes=[("wf", b, s)])
                    P.op("pool", lambda e, b=b, s=s, gw=gw: e.tensor_copy(wb[b][:, 8 * s:8 * s + 8, 0:gw], wf[b][:, 8 * s:8 * s + 8, 0:gw]),
                         reads=[("wf", b, s)], writes=[("wb", b, s)])
                if f["mode"] == "fm":
                    for c0 in range(0, gw, 128):
                        ch = (g0 + c0) // 128
                        for (t0, tn) in tblocks:
                            a = ai % 3
                            ai += 1
                            for kc in range(16):
                                P.op("pe", lambda e, a=a, b=b, kc=kc, c0=c0, t0=t0, tn=tn: e.matmul(
                                    acc[a][:, 0:tn], wb[b][:, kc, c0:c0 + 128], hT[:, kc, t0:t0 + tn], start=(kc == 0), stop=(kc == 15)),
                                    reads=[("wb", b, kc // 8), "hT"], writes=[("acc", a)])
                            o = oi % 2
                            oi += 1
                            ot = of[o] if odt == F32 else ob[o]
                            okey = ("of", o) if odt == F32 else ("ob", o)
                            if fi in gains:
                                n = ni % 2
                                ni += 1
                                P.op("act", lambda e, a=a, n=n, tn=tn: e.activation(sq[n][:, 0:tn], acc[a][:, 0:tn], AF.Square),
                                     reads=[("acc", a)], writes=[("sq", n)])
                                P.op("pe", lambda e, n=n, tn=tn: e.matmul(nps[n][:, 0:tn], onesb[:], sq[n][:, 0:tn], start=True, stop=True),
                                     reads=[("sq", n), "ones"], writes=[("nps", n)])
                                P.op("act", lambda e, n=n, tn=tn: e.activation(rs[n][:, 0:tn], nps[n][:, 0:tn], AF.Sqrt, scale=1.0 / 128, bias=EPS),
                                     reads=[("nps", n)], writes=[("rs", n)])
                                P.op("dve", lambda e, n=n, tn=tn: e.reciprocal(rs[n][:, 0:tn], rs[n][:, 0:tn]),
                                     reads=[("rs", n)], writes=[("rs", n)])
                                P.op("dve", lambda e, a=a, n=n, tn=tn, ot=ot, fi=fi: e.scalar_tensor_tensor(
                                    ot[:, 0:tn], acc[a][:, 0:tn], gains[fi][:, 0:1], rs[n][:, 0:tn], ALU.mult, ALU.mult),
                                    reads=[("acc", a), ("rs", n), ("gain", fi)], writes=[okey])
                            else:
                                P.op("act", lambda e, a=a, tn=tn, ot=ot: e.copy(ot[:, 0:tn], acc[a][:, 0:tn]),
                                     reads=[("acc", a)], writes=[okey])
                            P.dma("act", out[ch * 128:(ch + 1) * 128, t0:t0 + tn], ot[:, 0:tn], reads=[okey], writes=[("out", fi)])
                else:
                    for tt in range(NT // 128):
                        a = ai % 3
                        ai += 1
                        for kc in range(16):
                            P.op("pe", lambda e, a=a, b=b, kc=kc, tt=tt, gw=gw: e.matmul(
                                acc[a][:, 0:gw], hT[:, kc, tt * 128:(tt + 1) * 128], wb[b][:, kc, 0:gw], start=(kc == 0), stop=(kc == 15)),
                                reads=[("wb", b, kc // 8), "hT"], writes=[("acc", a)])
                        o = oi % 2
                        oi += 1
                        ot = of[o] if odt == F32 else ob[o]
                        okey = ("of", o) if odt == F32 else ("ob", o)
                        P.op("act", lambda e, a=a, gw=gw, ot=ot: e.copy(ot[:, 0:gw], acc[a][:, 0:gw]),
                             reads=[("acc", a)], writes=[okey])
                        P.dma("act", out[tt * 128:(tt + 1) * 128, g0:g0 + gw], ot[:, 0:gw], reads=[okey], writes=[("out", fi)])


def phase_outproj(nc, YT, KC, W_out, segs, modD, tag):
    with Phase(nc, "out" + tag) as ph:
        P = ph.P
        rows = sorted(set(s[2] for s in segs))
        gt = {r: ph.sb([128, 2048], F32, "gate") for r in rows}
        for r in rows:
            P.dma("sp", gt[r][:], modD[r, 4096:6144].partition_broadcast(128), writes=[("gate", r)])
        wf = [ph.sb([128, 8, 512], F32, "wf") for _ in range(2)]
        wb = [ph.sb([128, KC, 512], BF16, "wb") for _ in range(2)]
        yt = [ph.sb([128, KC, 128], BF16, "yt") for _ in range(3)]
        xt = [ph.sb([128, 512], F32, "xt") for _ in range(3)]
        t1 = [ph.sb([128, 512], F32, "t1") for _ in range(2)]
        ot = [ph.sb([128, 512], F32, "ot") for _ in range(2)]
        acc = [ph.ps() for _ in range(3)]
        ysrc = YT.rearrange("(c p) n -> p c n", p=128)
        wi = 0
        it = 0
        for db in range(4):
            b = db % 2
            wsrc = W_out[:, db * 512:(db + 1) * 512].rearrange("(c p) n -> p c n", p=128)
            for s in range(KC // 8):
                f = wi % 2
                wi += 1
                for h in range(2):
                    P.dma("sp", wf[f][:, 4 * h:4 * h + 4, :], wsrc[:, 8 * s + 4 * h:8 * s + 4 * h + 4, :], writes=[("wf", f, h)])
                P.op("pool", lambda e, f=f, b=b, s=s: e.tensor_copy(wb[b][:, 8 * s:8 * s + 8, :], wf[f][:]),
                     reads=[("wf", f, 0), ("wf", f, 1)], writes=[("wb", b, s)])
            for (xin, xout, r, n, tok0) in segs:
                for ti in range(n // 128):
                    y = it % 3
                    a = it % 3
                    o = it % 2
                    it += 1
                    t = tok0 + ti * 128
                    P.dma("sp", yt[y][:], ysrc[:, :, t:t + 128], writes=[("yt", y)])
                    P.dma("sp", xt[y][:], xin[ti * 128:(ti + 1) * 128, db * 512:(db + 1) * 512], writes=[("xt", y)])
                    for kc in range(KC):
                        P.op("pe", lambda e, a=a, y=y, b=b, kc=kc: e.matmul(acc[a][:], yt[y][:, kc, :], wb[b][:, kc, :],
                                                                         start=(kc == 0), stop=(kc == KC - 1)),
                             reads=[("yt", y), ("wb", b, kc // 8)], writes=[("acc", a)])
                    P.op("dve", lambda e, a=a, o=o, r=r, db=db: e.tensor_tensor(t1[o][:], acc[a][:], gt[r][:, db * 512:(db + 1) * 512], ALU.mult),
                         reads=[("acc", a), ("gate", r)], writes=[("t1", o)])
                    P.op("pool", lambda e, o=o, y=y: e.tensor_tensor(ot[o][:], t1[o][:], xt[y][:], ALU.add),
                         reads=[("t1", o), ("xt", y)], writes=[("ot", o)])
                    P.dma("act", xout[ti * 128:(ti + 1) * 128, db * 512:(db + 1) * 512], ot[o][:], reads=[("ot", o)], writes=["xout"])


def phase_convgate(nc, PT, NTA, seqs, cwT, flags, YT, tag):
    with Phase(nc, "cg" + tag) as ph:
        P = ph.P
        cw = ph.sb([128, 16, 3], F32, "cw")
        fl = ph.sb([128, 2], F32, "fl")
        P.dma("sp", cw[:], cwT, writes=["cw"])
        P.dma("sp", fl[:], flags, writes=["fl"])
        inb = [[ph.sb([128, NTA], F32, "in") for _ in range(4)] for _ in range(2)]
        nmax = max(n for _, n, _ in seqs)
        u = [ph.sb([128, nmax + 2], F32, "u") for _ in range(2)]
        v = [ph.sb([128, nmax], F32, "v") for _ in range(2)]
        sg = [ph.sb([128, nmax], F32, "sg") for _ in range(2)]
        yo = [ph.sb([128, nmax], BF16, "yo") for _ in range(2)]
        it = 0
        for c in range(16):
            b = c % 2
            for k in range(4):
                P.dma("sp", inb[b][k][:], PT[k * 2048 + c * 128:k * 2048 + (c + 1) * 128, :], writes=[("in", b, k)])
            bg, cg, hv, g = inb[b]
            for (tok0, n, halo) in seqs:
                i = it % 2
                it += 1
                rd = [("in", b, 1), ("in", b, 2)]
                P.op("pool", lambda e, i=i, n=n, tok0=tok0, cg=cg, hv=hv: e.tensor_tensor(u[i][:, 1:n + 1], cg[:, tok0:tok0 + n], hv[:, tok0:tok0 + n], ALU.mult),
                     reads=rd, writes=[("u", i)])
                if halo is None:
                    P.op("pool", lambda e, i=i: e.memset(u[i][:, 0:1], 0.0), writes=[("ul", i)])
                    P.op("pool", lambda e, i=i, n=n: e.memset(u[i][:, n + 1:n + 2], 0.0), writes=[("ur", i)])
                else:
                    tb, ta = halo
                    P.op("dve", lambda e, i=i, tb=tb, cg=cg, hv=hv: e.scalar_tensor_tensor(u[i][:, 0:1], cg[:, tb:tb + 1], fl[:, 0:1], hv[:, tb:tb + 1], ALU.mult, ALU.mult),
                         reads=rd + ["fl"], writes=[("ul", i)])
                    P.op("dve", lambda e, i=i, ta=ta, n=n, cg=cg, hv=hv: e.scalar_tensor_tensor(u[i][:, n + 1:n + 2], cg[:, ta:ta + 1], fl[:, 1:2], hv[:, ta:ta + 1], ALU.mult, ALU.mult),
                         reads=rd + ["fl"], writes=[("ur", i)])
                uk = [("u", i), ("ul", i), ("ur", i)]
                P.op("dve", lambda e, i=i, n=n, c=c: e.tensor_scalar(v[i][:, 0:n], u[i][:, 0:n], cw[:, c, 0:1], None, ALU.mult),
                     reads=uk + ["cw"], writes=[("v", i)])
                P.op("dve", lambda e, i=i, n=n, c=c: e.scalar_tensor_tensor(v[i][:, 0:n], u[i][:, 1:n + 1], cw[:, c, 1:2], v[i][:, 0:n], ALU.mult, ALU.add),
                     reads=uk + ["cw", ("v", i)], writes=[("v", i)])
                P.op("dve", lambda e, i=i, n=n, c=c: e.scalar_tensor_tensor(v[i][:, 0:n], u[i][:, 2:n + 2], cw[:, c, 2:3], v[i][:, 0:n], ALU.mult, ALU.add),
                     reads=uk + ["cw", ("v", i)], writes=[("v", i)])
                P.op("act", lambda e, i=i, n=n, tok0=tok0, g=g: e.activation(sg[i][:, 0:n], g[:, tok0:tok0 + n], AF.Silu),
                     reads=[("in", b, 3)], writes=[("sg", i)])
                P.op("pool", lambda e, i=i, n=n, tok0=tok0, bg=bg: e.tensor_tensor(sg[i][:, 0:n], sg[i][:, 0:n], bg[:, tok0:tok0 + n], ALU.mult),
                     reads=[("sg", i), ("in", b, 0)], writes=[("sg", i)])
                P.op("dve", lambda e, i=i, n=n: e.tensor_tensor(yo[i][:, 0:n], v[i][:, 0:n], sg[i][:, 0:n], ALU.mult),
                     reads=[("v", i), ("sg", i)], writes=[("yo", i)])
                P.dma("act", YT[c * 128:(c + 1) * 128, tok0:tok0 + n], yo[i][:, 0:n], reads=[("yo", i)], writes=["YT"])


def _dt(nc, n, s, d=F32, k="Internal"):
    return nc.dram_tensor(n, list(s), d, kind=k).ap()


def build_odd(T, with_ctx):
    nc = bass.Bass("TRN2", target_bir_lowering=False)
    I = lambda n, s, d=F32: _dt(nc, n, s, d, "ExternalInput")
    O = lambda n, s, d=F32: _dt(nc, n, s, d, "ExternalOutput")
    xh = I("xh", [T + 128, 2048])
    c2T = I("c2T", [128, 16, 2]); ada_w = I("ada_w", [2048, 6144]); ada_b = I("ada_b", [6144]); norm_g = I("norm_g", [2048])
    w_in = I("w_in", [2048, 8192]); cwT = I("cwT", [128, 16, 3]); flags = I("flags", [128, 2]); w_out = I("w_out", [2048, 2048])
    ident = I("ident", [128, 128], BF16)
    x_out = O("x_out", [T, 2048])
    NTA = T + 128 + (CTX if with_ctx else 0)
    if with_ctx:
        ctx = I("ctx", [CTX, 2048]); ctx_out = O("ctx_out", [CTX, 2048])
    modD = _dt(nc, "modD", [2, 6144]); HT = _dt(nc, "HT", [2048, NTA], BF16)
    PT = _dt(nc, "PT", [8192, NTA]); YT = _dt(nc, "YT", [2048, NTA], BF16)
    phase_mod(nc, c2T, ada_w, ada_b, norm_g, modD, "o")
    segs = [(xh, 0, T + 128)] + ([(ctx, 1, CTX)] if with_ctx else [])
    phase_norm(nc, segs, modD, HT, ident, "o")
    phase_proj(nc, HT, NTA, [dict(mode="fm", W=w_in, col0=0, ncols=8192, out=PT, odt=F32, norm=None)], "o")
    seqs = [(0, T, (T, T + 1))] + ([(T + 128, CTX, None)] if with_ctx else [])
    phase_convgate(nc, PT, NTA, seqs, cwT, flags, YT, "o")
    osegs = [(xh, x_out, 0, T, 0)] + ([(ctx, ctx_out, 1, CTX, T + 128)] if with_ctx else [])
    phase_outproj(nc, YT, 16, w_out, osegs, modD, "o")
    sempool(nc).close()
    return nc


def phase_convsilu(nc, PX, T, cw5T, cb5T, flags, ident_d, XCT, XS, BM, tag):
    NTL = T + CTX
    W = T + 4 + CTX + 4
    with Phase(nc, "cs" + tag) as ph:
        P = ph.P
        cw = ph.sb([128, 32, 5], F32, "cw")
        cb = ph.sb([128, 32], F32, "cb")
        fl = ph.sb([128, 2], F32, "fl")
        ident = ph.sb([128, 128], BF16, "id")
        P.dma("sp", cw[:], cw5T, writes=["cw"])
        P.dma("sp", cb[:], cb5T, writes=["cb"])
        P.dma("sp", fl[:], flags, writes=["fl"])
        P.dma("sp", ident[:], ident_d, writes=["ident"])
        u = [ph.sb([128, W], F32, "u") for _ in range(2)]
        acc = [ph.sb([128, NTL], F32, "acc") for _ in range(2)]
        xc = [ph.sb([128, NTL], BF16, "xc") for _ in range(2)]
        tb = [ph.sb([128, NTL // 128, 128], BF16, "tb") for _ in range(2)]
        pT = [ph.ps([128, 8, 128], BF16) for _ in range(2)]
        c0 = T + 4
        for b in range(2):
            P.op("pool", lambda e, b=b: e.memset(u[b][:, c0:c0 + 2], 0.0), writes=[("uz", b)])
            P.op("pool", lambda e, b=b: e.memset(u[b][:, c0 + 2 + CTX:c0 + 4 + CTX], 0.0), writes=[("uz", b)])
        pi = 0
        for c in range(32):
            b = c % 2
            row = PX[c * 128:(c + 1) * 128, :]
            P.dma("sp", u[b][:, 2:T + 2], row[:, 0:T], writes=[("u", b)])
            P.dma("sp", u[b][:, 0:2], row[:, T + 254:T + 256], writes=[("ul", b)])
            P.dma("sp", u[b][:, T + 2:T + 4], row[:, T + 256:T + 258], writes=[("ur", b)])
            P.dma("sp", u[b][:, c0 + 2:c0 + 2 + CTX], row[:, T + 512:T + 512 + CTX], writes=[("uc", b)])
            P.op("dve", lambda e, b=b: e.tensor_scalar(u[b][:, 0:2], u[b][:, 0:2], fl[:, 0:1], None, ALU.mult),
                 reads=[("ul", b), "fl"], writes=[("ul", b)])
            P.op("dve", lambda e, b=b: e.tensor_scalar(u[b][:, T + 2:T + 4], u[b][:, T + 2:T + 4], fl[:, 1:2], None, ALU.mult),
                 reads=[("ur", b), "fl"], writes=[("ur", b)])
            for (o0, n, a0) in ((0, T, 0), (c0, CTX, T)):
                rd = [("u", b), ("ul", b), ("ur", b), ("uc", b), ("uz", b), "cw"]
                P.op("dve", lambda e, b=b, c=c, o0=o0, n=n, a0=a0: e.tensor_scalar(acc[b][:, a0:a0 + n], u[b][:, o0:o0 + n], cw[:, c, 0:1], None, ALU.mult),
                     reads=rd, writes=[("acc", b, a0)])
                for k in range(1, 5):
                    P.op("dve", lambda e, b=b, c=c, o0=o0, n=n, a0=a0, k=k: e.scalar_tensor_tensor(
                        acc[b][:, a0:a0 + n], u[b][:, o0 + k:o0 + k + n], cw[:, c, k:k + 1], acc[b][:, a0:a0 + n], ALU.mult, ALU.add),
                        reads=rd + [("acc", b, a0)], writes=[("acc", b, a0)])
            P.op("act", lambda e, b=b, c=c: e.activation(xc[b][:], acc[b][:], AF.Silu, bias=cb[:, c:c + 1]),
                 reads=[("acc", b, 0), ("acc", b, T), "cb"], writes=[("xc", b)])
            P.dma("act", XCT[c * 128:(c + 1) * 128, :], xc[b][:], reads=[("xc", b)], writes=["XCT"])
            if c < 24:
                nt = NTL // 128
                for t0 in range(0, nt, 8):
                    p = pi % 2
                    pi += 1
                    tn = min(8, nt - t0)
                    for j in range(tn):
                        P.op("pe", lambda e, b=b, p=p, j=j, t0=t0: e.transpose(pT[p][:, j, :], xc[b][:, (t0 + j) * 128:(t0 + j + 1) * 128], ident[:]),
                             reads=[("xc", b), "ident"], writes=[("pT", p)])
                    P.op("act", lambda e, b=b, p=p, t0=t0, tn=tn: e.copy(tb[b][:, t0:t0 + tn, :], pT[p][:, 0:tn, :]),
                         reads=[("pT", p)], writes=[("tb", b)])
                dst = XS[:, c * 128:(c + 1) * 128] if c < 16 else BM[:, (c - 16) * 128:(c - 15) * 128]
                P.dma("act", dst.rearrange("(t p) n -> p t n", p=128), tb[b][:], reads=[("tb", b)], writes=["XSBM"])


def phase_ssd(nc, T, XS, BM, XCT, DTR, consts, a_log, dt_bias, tag, mode,
              Sseg=None, Dseg=None, Sall=None, Dall=None, mk=None, YF=None, YB=None, nseg=NSEG):
    with Phase(nc, "ssd" + tag) as ph:
        P = ph.P
        cst_ = ph.sb([128, 4, 128], F32, "consts")
        P.dma("sp", cst_[:], consts.rearrange("k p n -> p k n"), writes=["consts"])
        tri = [cst_[:, 0, :], cst_[:, 1, :]]
        Ls = [cst_[:, 2, :], cst_[:, 3, :]]
        ones = ph.sb([128, 128], F32, "ones")
        P.op("pool", lambda e: e.memset(ones[:], 1.0), writes=["ones"])
        aB = ph.sb([128, 64], F32, "aB")
        dtbB = ph.sb([128, 64], F32, "dtbB")
        P.dma("sp", aB[:], a_log.partition_broadcast(128), writes=["aB"])
        P.dma("sp", dtbB[:], dt_bias.partition_broadcast(128), writes=["dtbB"])
        P.op("act", lambda e: e.activation(aB[:], aB[:], AF.Exp), reads=["aB"], writes=["aB"])
        P.op("dve", lambda e: e.tensor_scalar(aB[:], aB[:], -1.0, None, ALU.mult), reads=["aB"], writes=["aB"])
        hT = [ph.sb([128, 8, 256], F32, "hT") for _ in range(2)]
        hTb = [ph.sb([128, 8, 256], BF16, "hTb") for _ in range(2)]
        dlog = [ph.sb([128, 32], F32, "dlog") for _ in range(2)]
        for d in range(2):
            P.op("pool", lambda e, d=d: e.memset(hT[d][:], 0.0), writes=[("hT", d, g) for g in range(8)])
            P.op("pool", lambda e, d=d: e.memset(hTb[d][:], 0.0), writes=[("hTb", d, g) for g in range(8)])
            P.op("pool", lambda e, d=d: e.memset(dlog[d][:], 0.0), writes=[("dlog", d)])
        xs = [ph.sb([128, 2048], BF16, "xs") for _ in range(2)]
        bm = [ph.sb([128, 1024], BF16, "bm") for _ in range(2)]
        bt = [ph.sb([128, 8, 128], BF16, "bt") for _ in range(2)]
        ct = [ph.sb([128, 8, 128], BF16, "ct") for _ in range(2)]
        dtr = [ph.sb([128, 32], F32, "dtr") for _ in range(2)]
        dtx = [ph.sb([128, 32], F32, "dtx") for _ in range(2)]
        dt = [ph.sb([128, 32], F32, "dt") for _ in range(2)]
        dta = [ph.sb([128, 32], F32, "dta") for _ in range(2)]
        cst = [ph.sb([128, 64], F32, "cst") for _ in range(2)]
        ecd = [ph.sb([128, 64], F32, "ecd") for _ in range(2)]
        tm = [ph.sb([128, 32], F32, "tm") for _ in range(2)]
        te = [ph.sb([128, 32], F32, "te") for _ in range(2)]
        yt = [ph.sb([128, 2048], F32, "yt") for _ in range(2)]
        cbm = [ph.sb([128, 128], F32, "cbm") for _ in range(2)]
        rseg = [ph.sb([128, 4, 128], F32, "rseg") for _ in range(2)]
        E = [ph.sb([128, 4, 128], F32, "E") for _ in range(2)]
        MT = [ph.sb([128, 4, 128], BF16, "MT") for _ in range(2)]
        xdt = [ph.sb([128, 4, 64], BF16, "xdt") for _ in range(2)]
        xw = [ph.sb([128, 4, 64], BF16, "xw") for _ in range(2)]
        ytmp = [ph.sb([128, 4, 64], F32, "ytmp") for _ in range(2)]
        ps_s = ph.ps()
        ps_cb = ph.ps()
        ps_seg = [ph.ps() for _ in range(2)]
        ps_y = ph.ps()
        ps_yi = ph.ps()
        ps_st2 = [ph.ps() for _ in range(2)]
        cnt = {"it": 0, "g": 0}
        Y = [YF, YB]

        def v4(ap):
            return ap.rearrange("p (r q) -> p r q", r=4)

        def chunk(tile, drow, d, full):
            b = cnt["it"] % 2
            cnt["it"] += 1
            tok = tile * 128
            P.dma("sp", xs[b][:], XS[tok:tok + 128, :], writes=[("xs", b)])
            P.dma("sp", bm[b][:], BM[tok:tok + 128, :], writes=[("bm", b)])
            if full:
                P.dma("sp", bt[b][:], XCT[2048:3072, tok:tok + 128].rearrange("(g p) n -> p g n", p=128), writes=[("bt", b)])
                P.dma("sp", ct[b][:], XCT[3072:4096, tok:tok + 128].rearrange("(g p) n -> p g n", p=128), writes=[("ct", b)])
            P.dma("sp", dtr[b][:], DTR[drow:drow + 128, d * 32:(d + 1) * 32], writes=[("dtr", b)])
            P.op("dve", lambda e: e.tensor_tensor(dtx[b][:], dtr[b][:], dtbB[:, d * 32:(d + 1) * 32], ALU.add),
                 reads=[("dtr", b), "dtbB"], writes=[("dtx", b)])
            P.op("act", lambda e: e.activation(dtx[b][:], dtx[b][:], AF.Exp), reads=[("dtx", b)], writes=[("dtx", b)])
            P.op("act", lambda e: e.activation(dt[b][:], dtx[b][:], AF.Ln, bias=1.0), reads=[("dtx", b)], writes=[("dt", b)])
            P.op("dve", lambda e: e.tensor_tensor(dta[b][:], dt[b][:], aB[:, d * 32:(d + 1) * 32], ALU.mult),
                 reads=[("dt", b), "aB"], writes=[("dta", b)])
            P.op("pe", lambda e: e.matmul(ps_s[:, 0:32], tri[d], dta[b][:], start=True, stop=True),
                 reads=[("dta", b), "consts"], writes=["ps_s"])
            P.op("pe", lambda e: e.matmul(ps_s[:, 32:64], ones[:], dta[b][:], start=True, stop=True),
                 reads=[("dta", b), "ones"], writes=["ps_s"])
            P.op("act", lambda e: e.copy(cst[b][:], ps_s[:, 0:64]), reads=["ps_s"], writes=[("cst", b)])
            P.op("act", lambda e: e.activation(ecd[b][:], cst[b][:], AF.Exp), reads=[("cst", b)], writes=[("ecd", b)])
            P.op("dve", lambda e: e.tensor_tensor(tm[b][:], cst[b][:, 32:64], cst[b][:, 0:32], ALU.subtract),
                 reads=[("cst", b)], writes=[("tm", b)])
            P.op("act", lambda e: e.activation(tm[b][:], tm[b][:], AF.Exp), reads=[("tm", b)], writes=[("tm", b)])
            P.op("dve", lambda e: e.tensor_tensor(te[b][:], tm[b][:], dt[b][:], ALU.mult),
                 reads=[("tm", b), ("dt", b)], writes=[("te", b)])
            if mode == "pre":
                P.op("dve", lambda e: e.tensor_tensor(dlog[d][:], dlog[d][:], cst[b][:, 32:64], ALU.add),
                     reads=[("dlog", d), ("cst", b)], writes=[("dlog", d)])
            for g in range(8):
                q = cnt["g"] % 2
                cnt["g"] += 1
                gs = slice(g * 4, g * 4 + 4)
                if full:
                    P.op("pe", lambda e, g=g: e.matmul(ps_cb[:, 0:128], bt[b][:, g, :], ct[b][:, g, :], start=True, stop=True),
                         reads=[("bt", b), ("ct", b)], writes=["ps_cb"])
                    P.op("dve", lambda e, q=q: e.tensor_tensor(cbm[q][:], ps_cb[:, 0:128], tri[d], ALU.mult),
                         reads=["ps_cb", "consts"], writes=[("cbm", q)])
                    P.op("pool", lambda e, q=q, gs=gs: e.tensor_tensor(
                        rseg[q][:], tri[d].unsqueeze(1).to_broadcast([128, 4, 128]),
                        dta[b][:, gs].unsqueeze(2).to_broadcast([128, 4, 128]), ALU.mult),
                        reads=[("dta", b), "consts"], writes=[("rseg", q)])
                    P.op("pe", lambda e, q=q: e.matmul(ps_seg[q][:], Ls[d], rseg[q][:].rearrange("p r i -> p (r i)"), start=True, stop=True),
                         reads=[("rseg", q), "consts"], writes=[("ps_seg", q)])
                    P.op("act", lambda e, q=q: e.activation(E[q][:].rearrange("p r i -> p (r i)"), ps_seg[q][:], AF.Exp),
                         reads=[("ps_seg", q)], writes=[("E", q)])
                    P.op("dve", lambda e, q=q: e.tensor_tensor(MT[q][:], E[q][:], cbm[q][:].unsqueeze(1).to_broadcast([128, 4, 128]), ALU.mult),
                         reads=[("E", q), ("cbm", q)], writes=[("MT", q)])
                    P.op("pool", lambda e, q=q, g=g, gs=gs: e.tensor_tensor(
                        xdt[q][:], v4(xs[b][:, g * 256:(g + 1) * 256]), dt[b][:, gs].unsqueeze(2).to_broadcast([128, 4, 64]), ALU.mult),
                        reads=[("xs", b), ("dt", b)], writes=[("xdt", q)])
                    yo = 0
                    for r in range(4):
                        P.op("pe", lambda e, q=q, r=r, yo=yo: e.matmul(ps_y[:, yo + r * 64:yo + (r + 1) * 64], MT[q][:, r, :], xdt[q][:, r, :], start=True, stop=True),
                             reads=[("MT", q), ("xdt", q)], writes=["ps_y"])
                    P.op("pe", lambda e, q=q, g=g, yo=yo: e.matmul(ps_yi[:, yo:yo + 256], ct[b][:, g, :], hTb[d][:, g, :], start=True, stop=True),
                         reads=[("ct", b), ("hTb", d, g)], writes=["ps_yi"])
                    P.op("dve", lambda e, q=q, gs=gs, yo=yo: e.tensor_tensor(
                        ytmp[q][:], v4(ps_yi[:, yo:yo + 256]), ecd[b][:, gs].unsqueeze(2).to_broadcast([128, 4, 64]), ALU.mult),
                        reads=["ps_yi", ("ecd", b)], writes=[("ytmp", q)])
                    P.op("dve", lambda e, q=q, g=g, yo=yo: e.tensor_tensor(
                        yt[b][:, g * 256:(g + 1) * 256], ytmp[q][:].rearrange("p r q -> p (r q)"), ps_y[:, yo:yo + 256], ALU.add),
                        reads=[("ytmp", q), "ps_y"], writes=[("yt", b, g)])
                P.op("pool", lambda e, q=q, g=g, gs=gs: e.tensor_tensor(
                    xw[q][:], v4(xs[b][:, g * 256:(g + 1) * 256]), te[b][:, gs].unsqueeze(2).to_broadcast([128, 4, 64]), ALU.mult),
                    reads=[("xs", b), ("te", b)], writes=[("xw", q)])
                so = 0
                ps_st = ps_st2[q]
                P.op("pe", lambda e, q=q, g=g, so=so, ps_st=ps_st: e.matmul(ps_st[:, so:so + 256], bm[b][:, g * 128:(g + 1) * 128],
                                                               xw[q][:].rearrange("p r q -> p (r q)"), start=True, stop=True),
                     reads=[("bm", b), ("xw", q)], writes=[("ps_st", q)])
                P.op("dve", lambda e, g=g, gs=gs: e.tensor_tensor(
                    v4(hT[d][:, g, :]), v4(hT[d][:, g, :]), ecd[b][:, 32 + g * 4:32 + g * 4 + 4].unsqueeze(2).to_broadcast([128, 4, 64]), ALU.mult),
                    reads=[("hT", d, g), ("ecd", b)], writes=[("hT", d, g)])
                P.op("dve", lambda e, g=g, so=so, ps_st=ps_st: e.tensor_tensor(hT[d][:, g, :], hT[d][:, g, :], ps_st[:, so:so + 256], ALU.add),
                     reads=[("hT", d, g), ("ps_st", q)], writes=[("hT", d, g)])
                P.op("act", lambda e, g=g: e.copy(hTb[d][:, g, :], hT[d][:, g, :]), reads=[("hT", d, g)], writes=[("hTb", d, g)])
            if full:
                P.dma("act", Y[d][tok:tok + 128, :], yt[b][:], reads=[("yt", b, g) for g in range(8)], writes=["Y"])

        nl = T // 128
        lat = [(t, t * 128) for t in range(nl)]
        cx = [(nl + j, T + 512 + j * 128) for j in range(CTX // 128)]
        if mode == "pre":
            for d in range(2):
                for (tile, drow) in (lat if d == 0 else lat[::-1]):
                    chunk(tile, drow, d, False)
                P.dma("act", Sseg[d], hT[d][:].rearrange("p g q -> p (g q)"), reads=[("hT", d, g) for g in range(8)], writes=["Sseg"])
                P.dma("act", Dseg[d], dlog[d][:], reads=[("dlog", d)], writes=["Dseg"])
        else:
            mkt = ph.sb([128, 2 * nseg], F32, "mk")
            P.dma("sp", mkt[:], mk, writes=["mk"])
            St = [ph.sb([128, 2048], F32, "St") for _ in range(2)]
            dl = [ph.sb([128, 32], F32, "dl") for _ in range(2)]
            ci = 0
            for d in range(2):
                for (tile, drow) in (cx if d == 0 else cx[::-1]):
                    chunk(tile, drow, d, True)
                allk = [("hT", d, g) for g in range(8)]
                for s in (range(nseg) if d == 0 else range(nseg - 1, -1, -1)):
                    c = ci % 2
                    ci += 1
                    col = d * nseg + s
                    P.dma("sp", St[c][:], Sall[s, d], writes=[("St", c)])
                    P.dma("sp", dl[c][:], Dall[s, d], writes=[("dl", c)])
                    P.op("dve", lambda e, c=c, col=col: e.tensor_scalar(dl[c][:], dl[c][:], mkt[:, col:col + 1], None, ALU.mult),
                         reads=[("dl", c), "mk"], writes=[("dl", c)])
                    P.op("act", lambda e, c=c: e.activation(dl[c][:], dl[c][:], AF.Exp), reads=[("dl", c)], writes=[("dl", c)])
                    P.op("dve", lambda e, c=c, d=d: e.tensor_tensor(
                        hT[d][:].rearrange("p g (r q) -> p (g r) q", r=4), hT[d][:].rearrange("p g (r q) -> p (g r) q", r=4),
                        dl[c][:].unsqueeze(2).to_broadcast([128, 32, 64]), ALU.mult),
                        reads=allk + [("dl", c)], writes=allk)
                    P.op("dve", lambda e, c=c, d=d, col=col: e.scalar_tensor_tensor(
                        hT[d][:].rearrange("p g q -> p (g q)"), St[c][:], mkt[:, col:col + 1], hT[d][:].rearrange("p g q -> p (g q)"), ALU.mult, ALU.add),
                        reads=allk + [("St", c), "mk"], writes=allk)
                P.op("act", lambda e, d=d: e.copy(hTb[d][:], hT[d][:]), reads=allk, writes=[("hTb", d, g) for g in range(8)])
                for (tile, drow) in (lat if d == 0 else lat[::-1]):
                    chunk(tile, drow, d, True)


def phase_gnorm(nc, T, YF, YB, XS, Z, d_skip, ssm_g, ident_d, YT, tag):
    NTL = T + CTX
    with Phase(nc, "gn" + tag) as ph:
        P = ph.P
        dB = ph.sb([128, 32], F32, "dB")
        gB = ph.sb([128, 2048], F32, "gB")
        ident = ph.sb([128, 128], BF16, "id")
        P.dma("sp", dB[:], d_skip.partition_broadcast(128), writes=["dB"])
        P.dma("sp", gB[:], ssm_g.partition_broadcast(128), writes=["gB"])
        P.dma("sp", ident[:], ident_d, writes=["ident"])
        yf = [ph.sb([128, 2048], F32, "yf") for _ in range(2)]
        yb = [ph.sb([128, 2048], F32, "yb") for _ in range(2)]
        xs = [ph.sb([128, 2048], BF16, "xs") for _ in range(2)]
        z = [ph.sb([128, 2048], F32, "z") for _ in range(2)]
        t1 = [ph.sb([128, 2048], F32, "t1") for _ in range(2)]
        yn = [ph.sb([128, 2048], BF16, "yn") for _ in range(2)]
        junk = [ph.sb([128, 256], BF16, "junk") for _ in range(2)]
        ss = [ph.sb([128, 16], F32, "ss") for _ in range(2)]
        ho = [ph.sb([128, 16, 128], BF16, "ho") for _ in range(2)]
        pT = [ph.ps([128, 8, 128], BF16) for _ in range(2)]
        v32 = lambda ap: ap.rearrange("p (h q) -> p h q", h=32)
        v8 = lambda ap: ap.rearrange("p (g q) -> p g q", g=8)
        for it in range(NTL // 128):
            b = it % 2
            tok = it * 128
            zrow = tok if tok < T else T + 512 + (tok - T)
            P.dma("sp", yf[b][:], YF[tok:tok + 128, :], writes=[("yf", b)])
            P.dma("sp", yb[b][:], YB[tok:tok + 128, :], writes=[("yb", b)])
            P.dma("sp", xs[b][:], XS[tok:tok + 128, :], writes=[("xs", b)])
            P.dma("sp", z[b][:], Z[zrow:zrow + 128, :], writes=[("z", b)])
            P.op("pool", lambda e, b=b: e.tensor_tensor(yf[b][:], yf[b][:], yb[b][:], ALU.add), reads=[("yf", b), ("yb", b)], writes=[("yf", b)])
            P.op("pool", lambda e, b=b: e.tensor_tensor(v32(t1[b][:]), v32(xs[b][:]), dB[:].unsqueeze(2).to_broadcast([128, 32, 64]), ALU.mult),
                 reads=[("xs", b), "dB"], writes=[("t1", b)])
            P.op("dve", lambda e, b=b: e.tensor_tensor(yf[b][:], yf[b][:], t1[b][:], ALU.add), reads=[("yf", b), ("t1", b)], writes=[("yf", b)])
            P.op("act", lambda e, b=b: e.activation(z[b][:], z[b][:], AF.Silu), reads=[("z", b)], writes=[("z", b)])
            P.op("dve", lambda e, b=b: e.tensor_tensor(yf[b][:], yf[b][:], z[b][:], ALU.mult), reads=[("yf", b), ("z", b)], writes=[("yf", b)])
            for g in range(8):
                P.op("act", lambda e, b=b, g=g: e.activation(junk[g % 2][:], yf[b][:, g * 256:(g + 1) * 256], AF.Square, accum_out=ss[b][:, g:g + 1]),
                     reads=[("yf", b)], writes=[("junk", g % 2), ("ss", b, g)])
            P.op("act", lambda e, b=b: e.activation(ss[b][:, 8:16], ss[b][:, 0:8], AF.Sqrt, scale=1.0 / 256, bias=EPS),
                 reads=[("ss", b, g) for g in range(8)], writes=[("sd", b)])
            P.op("dve", lambda e, b=b: e.reciprocal(ss[b][:, 8:16], ss[b][:, 8:16]), reads=[("sd", b)], writes=[("sd", b)])
            P.op("dve", lambda e, b=b: e.tensor_tensor(v8(t1[b][:]), v8(yf[b][:]), ss[b][:, 8:16].unsqueeze(2).to_broadcast([128, 8, 256]), ALU.mult),
                 reads=[("yf", b), ("sd", b)], writes=[("t1", b)])
            P.op("pool", lambda e, b=b: e.tensor_tensor(yn[b][:], t1[b][:], gB[:], ALU.mult), reads=[("t1", b), "gB"], writes=[("yn", b)])
            for half in range(2):
                for j in range(8):
                    kc = half * 8 + j
                    P.op("pe", lambda e, b=b, kc=kc, half=half, j=j: e.transpose(pT[half][:, j, :], yn[b][:, kc * 128:(kc + 1) * 128], ident[:]),
                         reads=[("yn", b), "ident"], writes=[("pT", half)])
                if half == 0:
                    P.op("act", lambda e, b=b: e.copy(ho[b][:, 0:8, :], pT[0][:]), reads=[("pT", 0)], writes=[("ho", b, 0)])
                else:
                    P.op("dve", lambda e, b=b: e.tensor_copy(ho[b][:, 8:16, :], pT[1][:]), reads=[("pT", 1)], writes=[("ho", b, 1)])
            P.dma("act", YT[2048:4096, tok:tok + 128].rearrange("(c p) n -> p c n", p=128), ho[b][:],
                  reads=[("ho", b, 0), ("ho", b, 1)], writes=["YT"])


def phase_na(nc, T, QT, KT, V, GT, biasD, YT, ctx_queries, tag):
    NTA = T + 768
    ROWS = T // 64
    NP = ROWS // 2
    SC = 128 ** -0.5
    with Phase(nc, "na" + tag) as ph:
        P = ph.P
        onesb = ph.sb([128, 128], BF16, "ones")
        P.op("pool", lambda e: e.memset(onesb[:], 1.0), writes=["ones"])
        kT = [ph.sb([128, NTA], BF16, "kT") for _ in range(2)]
        qT = [ph.sb([128, NTA], BF16, "qT") for _ in range(2)]
        gT = [ph.sb([128, NTA], F32, "gT") for _ in range(2)]
        vt = [ph.sb([128, NTA // 128, 128], BF16, "v") for _ in range(2)]
        bia = [ph.sb([128, 25, 128], F32, "bia") for _ in range(2)]
        yh = [ph.sb([128, T + CTX], BF16, "yh") for _ in range(2)]
        s_sb = [ph.sb([128, 640], F32, "s") for _ in range(2)]
        pl = [ph.sb([128, 896], BF16, "pl") for _ in range(2)]
        rd = [ph.sb([128, 128], F32, "rd") for _ in range(2)]
        o1 = [ph.sb([128, 128], F32, "o1") for _ in range(2)]
        sg = [ph.sb([128, 128], F32, "sg") for _ in range(2)]
        ps_a = [ph.ps() for _ in range(2)]
        ps_b = [ph.ps() for _ in range(2)]
        ps_o = [ph.ps() for _ in range(2)]
        ps_d = [ph.ps() for _ in range(2)]

        def tokoff(p):
            if 2 * p < 4:
                return T + 128 * p
            if 2 * p < 4 + ROWS:
                return 128 * p - 256
            return T + 256 + 128 * (p - 2 - NP)

        cx0 = T + 512
        ui = 0
        for h in range(16):
            hb = h % 2
            P.dma("sp", kT[hb][:], KT[h * 128:(h + 1) * 128, :], writes=[("kT", hb)])
            P.dma("sp", qT[hb][:], QT[h * 128:(h + 1) * 128, :], writes=[("qT", hb)])
            P.dma("sp", gT[hb][:], GT[h * 128:(h + 1) * 128, :], writes=[("gT", hb)])
            P.dma("sp", vt[hb][:], V[:, h * 128:(h + 1) * 128].rearrange("(t p) c -> p t c", p=128), writes=[("v", hb)])
            P.dma("sp", bia[hb][:], biasD[h], writes=[("bia", hb)])
            units = []
            for lp in range(NP):
                var = 1 if lp == 0 else 2 if lp == 1 else 3 if lp == NP - 2 else 4 if lp == NP - 1 else 0
                keys = [tokoff(lp + t) for t in range(5)] + [cx0, cx0 + 128]
                units.append((128 * lp, 128 * lp, keys, var, 5))
            if ctx_queries:
                for j in range(2):
                    units.append((cx0 + 128 * j, T + 128 * j, [cx0, cx0 + 128], None, 0))
            for (q0, y0, keys, var, nlat) in units:
                u = ui % 2
                ui += 1
                nk = len(keys)
                rdk = [("kT", hb), ("qT", hb)]
                for t, k0 in enumerate(keys):
                    dst = ps_a[u][:, t * 128:(t + 1) * 128] if t < 4 else ps_b[u][:, (t - 4) * 128:(t - 3) * 128]
                    P.op("pe", lambda e, dst=dst, k0=k0, q0=q0, hb=hb: e.matmul(dst, kT[hb][:, k0:k0 + 128], qT[hb][:, q0:q0 + 128], start=True, stop=True),
                         reads=rdk, writes=[("ps_a", u) if t < 4 else ("ps_b", u)])
                if nlat:
                    P.op("dve", lambda e, u=u, hb=hb, var=var: e.scalar_tensor_tensor(
                        s_sb[u][:, 0:512], ps_a[u][:], SC, bia[hb][:, var * 5:var * 5 + 4, :].rearrange("p t q -> p (t q)"), ALU.mult, ALU.add),
                        reads=[("ps_a", u), ("bia", hb)], writes=[("s", u, 0)])
                    P.op("dve", lambda e, u=u, hb=hb, var=var: e.scalar_tensor_tensor(
                        s_sb[u][:, 512:640], ps_b[u][:, 0:128], SC, bia[hb][:, var * 5 + 4, :], ALU.mult, ALU.add),
                        reads=[("ps_b", u), ("bia", hb)], writes=[("s", u, 1)])
                    P.op("act", lambda e, u=u: e.activation(pl[u][:, 0:640], s_sb[u][:], AF.Exp),
                         reads=[("s", u, 0), ("s", u, 1)], writes=[("pl", u, 0)])
                    P.op("act", lambda e, u=u: e.activation(pl[u][:, 640:896], ps_b[u][:, 128:384], AF.Exp, scale=SC),
                         reads=[("ps_b", u)], writes=[("pl", u, 1)])
                else:
                    P.op("act", lambda e, u=u: e.activation(pl[u][:, 0:256], ps_a[u][:, 0:256], AF.Exp, scale=SC),
                         reads=[("ps_a", u)], writes=[("pl", u, 0), ("pl", u, 1)])
                rp = [("pl", u, 0), ("pl", u, 1)]
                for t, k0 in enumerate(keys):
                    P.op("pe", lambda e, u=u, t=t, k0=k0, hb=hb, nk=nk: e.matmul(ps_o[u][:, 0:128], vt[hb][:, k0 // 128, :], pl[u][:, t * 128:(t + 1) * 128],
                                                                             start=(t == 0), stop=(t == nk - 1)),
                         reads=rp + [("v", hb)], writes=[("ps_o", u)])
                for t, k0 in enumerate(keys):
                    P.op("pe", lambda e, u=u, t=t, nk=nk: e.matmul(ps_d[u][:, 0:128], onesb[:], pl[u][:, t * 128:(t + 1) * 128],
                                                                  start=(t == 0), stop=(t == nk - 1)),
                         reads=rp + ["ones"], writes=[("ps_d", u)])
                P.op("dve", lambda e, u=u: e.reciprocal(rd[u][:], ps_d[u][:, 0:128]), reads=[("ps_d", u)], writes=[("rd", u)])
                P.op("dve", lambda e, u=u: e.tensor_tensor(o1[u][:], ps_o[u][:, 0:128], rd[u][:], ALU.mult),
                     reads=[("ps_o", u), ("rd", u)], writes=[("o1", u)])
                P.op("act", lambda e, u=u, hb=hb, q0=q0: e.activation(sg[u][:], gT[hb][:, q0:q0 + 128], AF.Silu),
                     reads=[("gT", hb)], writes=[("sg", u)])
                P.op("pool", lambda e, u=u, hb=hb, y0=y0: e.tensor_tensor(yh[hb][:, y0:y0 + 128], o1[u][:], sg[u][:], ALU.mult),
                     reads=[("o1", u), ("sg", u)], writes=[("yh", hb, y0)])
            ncol = T + (CTX if ctx_queries else 0)
            P.dma("act", YT[h * 128:(h + 1) * 128, 0:ncol], yh[hb][:, 0:ncol],
                  reads=[("yh", hb, y0) for (_, y0, _, _, _) in units], writes=["YT"])


def _even_common(nc, T):
    I = lambda n, s, d=F32: _dt(nc, n, s, d, "ExternalInput")
    D = {}
    D["xh"] = I("xh", [T + 512, 2048]); D["ctx"] = I("ctx", [CTX, 2048])
    D["c2T"] = I("c2T", [128, 16, 2]); D["ada_w"] = I("ada_w", [2048, 6144]); D["ada_b"] = I("ada_b", [6144])
    D["norm_g"] = I("norm_g", [2048]); D["w_in"] = I("w_in", [2048, EVEN_IN])
    D["cw5T"] = I("cw5T", [128, 32, 5]); D["cb5T"] = I("cb5T", [128, 32]); D["flags"] = I("flags", [128, 2])
    D["consts"] = I("consts", [4, 128, 128]); D["a_log"] = I("a_log", [64]); D["dt_bias"] = I("dt_bias", [64])
    D["ident"] = I("ident", [128, 128], BF16)
    NTA = T + 768
    NTL = T + CTX
    D["modD"] = _dt(nc, "modD", [2, 6144]); D["HT"] = _dt(nc, "HT", [2048, NTA], BF16)
    D["PX"] = _dt(nc, "PX", [4096, NTA]); D["DTR"] = _dt(nc, "DTR", [NTA, 64])
    D["XCT"] = _dt(nc, "XCT", [4096, NTL], BF16); D["XS"] = _dt(nc, "XS", [NTL, 2048], BF16); D["BM"] = _dt(nc, "BM", [NTL, 1024], BF16)
    return D, NTA, NTL


def build_pre(T):
    nc = bass.Bass("TRN2", target_bir_lowering=False)
    D, NTA, NTL = _even_common(nc, T)
    Sseg = _dt(nc, "Sseg", [2, 128, 2048], F32, "ExternalOutput")
    Dseg = _dt(nc, "Dseg", [2, 128, 32], F32, "ExternalOutput")
    phase_mod(nc, D["c2T"], D["ada_w"], D["ada_b"], D["norm_g"], D["modD"], "p")
    phase_norm(nc, [(D["xh"], 0, T + 512), (D["ctx"], 1, CTX)], D["modD"], D["HT"], D["ident"], "p")
    fams = [dict(mode="fm", W=D["w_in"], col0=COL_XBC, ncols=4096, out=D["PX"], odt=F32, norm=None),
            dict(mode="tm", W=D["w_in"], col0=COL_DT, ncols=64, out=D["DTR"], odt=F32)]
    phase_proj(nc, D["HT"], NTA, fams, "p")
    import os
    stop = int(os.environ.get("K_STOP", "9"))
    if stop >= 1:
        phase_convsilu(nc, D["PX"], T, D["cw5T"], D["cb5T"], D["flags"], D["ident"], D["XCT"], D["XS"], D["BM"], "p")
    if stop >= 2:
        phase_ssd(nc, T, D["XS"], D["BM"], D["XCT"], D["DTR"], D["consts"], D["a_log"], D["dt_bias"], "p", "pre", Sseg=Sseg, Dseg=Dseg)
    sempool(nc).close()
    return nc


def build_even(T, update_ctx, nseg):
    nc = bass.Bass("TRN2", target_bir_lowering=False)
    D, NTA, NTL = _even_common(nc, T)
    I = lambda n, s, d=F32: _dt(nc, n, s, d, "ExternalInput")
    O = lambda n, s, d=F32: _dt(nc, n, s, d, "ExternalOutput")
    Sall = I("Sall", [nseg, 2, 128, 2048]); Dall = I("Dall", [nseg, 2, 128, 32]); mk = I("mk", [128, 2 * nseg])
    d_skip = I("d_skip", [32]); ssm_g = I("ssm_g", [2048]); qg = I("qg", [128, 1]); kg = I("kg", [128, 1])
    biasD = I("biasD", [16, 128, 25, 128]); w_out = I("w_out", [4096, 2048])
    x_out = O("x_out", [T, 2048])
    ctx_out = O("ctx_out", [CTX, 2048]) if update_ctx else None
    QT = _dt(nc, "QT", [2048, NTA], BF16); KT = _dt(nc, "KT", [2048, NTA], BF16); GT = _dt(nc, "GT", [2048, NTA])
    V = _dt(nc, "V", [NTA, 2048], BF16); Z = _dt(nc, "Z", [NTA, 2048])
    YF = _dt(nc, "YF", [NTL, 2048]); YB = _dt(nc, "YB", [NTL, 2048]); YT = _dt(nc, "YT", [4096, NTL], BF16)
    w_in = D["w_in"]
    phase_mod(nc, D["c2T"], D["ada_w"], D["ada_b"], D["norm_g"], D["modD"], "e")
    phase_norm(nc, [(D["xh"], 0, T + 512), (D["ctx"], 1, CTX)], D["modD"], D["HT"], D["ident"], "e")
    fams = [dict(mode="fm", W=w_in, col0=COL_Q, ncols=2048, out=QT, odt=BF16, norm=qg),
            dict(mode="fm", W=w_in, col0=COL_K, ncols=2048, out=KT, odt=BF16, norm=kg),
            dict(mode="fm", W=w_in, col0=COL_GATE, ncols=2048, out=GT, odt=F32, norm=None),
            dict(mode="fm", W=w_in, col0=COL_XBC, ncols=4096, out=D["PX"], odt=F32, norm=None),
            dict(mode="tm", W=w_in, col0=COL_V, ncols=2048, out=V, odt=BF16),
            dict(mode="tm", W=w_in, col0=COL_Z, ncols=2048, out=Z, odt=F32),
            dict(mode="tm", W=w_in, col0=COL_DT, ncols=64, out=D["DTR"], odt=F32)]
    phase_proj(nc, D["HT"], NTA, fams, "e")
    phase_convsilu(nc, D["PX"], T, D["cw5T"], D["cb5T"], D["flags"], D["ident"], D["XCT"], D["XS"], D["BM"], "e")
    phase_ssd(nc, T, D["XS"], D["BM"], D["XCT"], D["DTR"], D["consts"], D["a_log"], D["dt_bias"], "e", "main",
              Sall=Sall, Dall=Dall, mk=mk, YF=YF, YB=YB, nseg=nseg)
    phase_gnorm(nc, T, YF, YB, D["XS"], Z, d_skip, ssm_g, D["ident"], YT, "e")
    phase_na(nc, T, QT, KT, V, GT, biasD, YT, update_ctx, "e")
    osegs = [(D["xh"], x_out, 0, T, 0)] + ([(D["ctx"], ctx_out, 1, CTX, T)] if update_ctx else [])
    phase_outproj(nc, YT, 32, w_out, osegs, D["modD"], "e")
    sempool(nc).close()
    return nc


def _bias_tables(rpb, R0, ROWS, RTOT):
    NP = ROWS // 2
    first, last = (R0 == 0), (R0 + ROWS == RTOT)
    lp_of = [min(2, NP - 3) if NP > 4 else 0, 0, 1, NP - 2, NP - 1]
    kr = np.arange(128) // 64
    kc = np.arange(128) % 64
    qr, qc = kr, kc
    out = np.empty((16, 128, 25, 128), np.float32)
    cs = np.clip(qc - 8, 0, 48)
    colok = (kc[:, None] >= cs[None, :]) & (kc[:, None] < cs[None, :] + 16)
    dc = np.clip(kc[:, None] - qc[None, :], -15, 15) + 15
    for v, lp in enumerate(lp_of):
        for t in range(5):
            e = 2 * (lp + t) + kr
            l = 2 * lp + qr
            o = e[:, None] - l[None, :]
            rowok = (o >= 0) & (o <= 7)
            loc = e - 4
            g = R0 + loc
            if first:
                g = np.where(e < 4, 4 + e, g)
            if last:
                g = np.where(loc >= ROWS, RTOT - 8 + (loc - ROWS), g)
            qrow = R0 + l
            dr = g[:, None] - qrow[None, :] + 7
            ok = rowok & colok & (dr >= 0) & (dr <= 14)
            drc = np.clip(dr, 0, 14)
            vals = rpb[:, drc, dc]
            out[:, :, v * 5 + t, :] = np.where(ok[None], vals, np.float32(NEG))
    return out


_CONSTS = None


def _consts():
    global _CONSTS
    if _CONSTS is None:
        t = np.arange(128)
        triF = (t[:, None] <= t[None, :]).astype(np.float32)
        triB = (t[:, None] >= t[None, :]).astype(np.float32)
        LsF = (t[:, None] > t[None, :]).astype(np.float32)
        LsB = (t[:, None] < t[None, :]).astype(np.float32)
        _CONSTS = np.ascontiguousarray(np.stack([triF, triB, LsF, LsB]))
    return _CONSTS


_PROGS = {}


def _prog(key, fn):
    if key not in _PROGS:
        _PROGS[key] = fn()
    return _PROGS[key]


def _pT(v, nchunk):
    k = v.shape[0]
    return np.ascontiguousarray(v.reshape(k, nchunk, 128).transpose(2, 1, 0))


def _run_model(inp, seq, nseg):
    import ml_dtypes
    B = inp["x"].shape[0]
    T = seq // nseg
    ROWS = T // GRID_W
    RTOT = seq // GRID_W
    ncores = B * nseg
    cores = [(b, s) for b in range(B) for s in range(nseg)]
    x = np.array(inp["x"], np.float32)
    ctx = np.array(inp["ctx"], np.float32)
    ident = np.eye(128, dtype=np.float32).astype(ml_dtypes.bfloat16)
    depth = inp["ada_w"].shape[0]
    for i in range(depth):
        update_ctx = any(j % 2 == 0 for j in range(i + 1, depth))
        c2T = [_pT(np.stack([inp["c"][b], inp["c_ctx"]]), 16) for b in range(B)]
        base = dict(ada_w=inp["ada_w"][i], ada_b=inp["ada_b"][i], norm_g=inp["norm_g"][i], ident=ident)
        if i % 2 == 0:
            e = i // 2
            xbc_w = inp["ssd_conv_w"][e]
            common = dict(base, w_in=inp["na_ssd_w_in"][e], cw5T=_pT(xbc_w, 32),
                          cb5T=np.ascontiguousarray(_pT(inp["ssd_conv_b"][e][None], 32)[:, :, 0]),
                          consts=_consts(), a_log=np.ascontiguousarray(inp["ssd_a_log"][e].reshape(64)),
                          dt_bias=np.ascontiguousarray(inp["ssd_dt_bias"][e].reshape(64)))
            maps = []
            for (b, s) in cores:
                R0 = s * ROWS
                own = x[b, s * T:(s + 1) * T]
                hb0 = (4 if s == 0 else R0 - 4) * GRID_W
                ha0 = (RTOT - 8 if s == nseg - 1 else R0 + ROWS) * GRID_W
                xh = np.concatenate([own, x[b, hb0:hb0 + 256], x[b, ha0:ha0 + 192], np.zeros((64, D_MODEL), np.float32)], 0)
                fl = np.zeros((128, 2), np.float32)
                fl[:, 0] = 0.0 if s == 0 else 1.0
                fl[:, 1] = 0.0 if s == nseg - 1 else 1.0
                maps.append(dict(common, xh=xh, ctx=ctx[b], c2T=c2T[b], flags=fl))
            pre = _prog(("pre", T), lambda: build_pre(T))
            r = run_bass_kernel_spmd(pre, maps, core_ids=list(range(ncores))).results
            Sall = [np.ascontiguousarray(np.stack([r[b * nseg + s]["Sseg"] for s in range(nseg)])) for b in range(B)]
            Dall = [np.ascontiguousarray(np.stack([r[b * nseg + s]["Dseg"] for s in range(nseg)])) for b in range(B)]
            for ci, (b, s) in enumerate(cores):
                mk = np.zeros((128, 2 * nseg), np.float32)
                mk[:, :s] = 1.0
                mk[:, nseg + s + 1:] = 1.0
                maps[ci].update(Sall=Sall[b], Dall=Dall[b], mk=mk, d_skip=inp["ssd_d"][e], ssm_g=inp["ssd_norm_g"][e],
                                qg=np.ascontiguousarray(inp["q_norm_g"][e][:, None]), kg=np.ascontiguousarray(inp["k_norm_g"][e][:, None]),
                                biasD=_bias_tables(inp["na_rpb"][e], s * ROWS, ROWS, RTOT), w_out=inp["na_ssd_w_out"][e])
            main = _prog(("even", T, update_ctx, nseg), lambda: build_even(T, update_ctx, nseg))
            r = run_bass_kernel_spmd(main, maps, core_ids=list(range(ncores))).results
        else:
            o = i // 2
            common = dict(base, w_in=inp["sc_w_in"][o], cwT=_pT(inp["sc_conv_w"][o], 16), w_out=inp["sc_w_out"][o])
            maps = []
            for (b, s) in cores:
                own = x[b, s * T:(s + 1) * T]
                halo = np.zeros((128, D_MODEL), np.float32)
                fl = np.zeros((128, 2), np.float32)
                if s > 0:
                    halo[0] = x[b, s * T - 1]
                    fl[:, 0] = 1.0
                if s < nseg - 1:
                    halo[1] = x[b, (s + 1) * T]
                    fl[:, 1] = 1.0
                m = dict(common, xh=np.concatenate([own, halo], 0), c2T=c2T[b], flags=fl)
                if update_ctx:
                    m["ctx"] = ctx[b]
                maps.append(m)
            prog = _prog(("odd", T, update_ctx), lambda: build_odd(T, update_ctx))
            r = run_bass_kernel_spmd(prog, maps, core_ids=list(range(ncores))).results
        xn = np.empty_like(x)
        for ci, (b, s) in enumerate(cores):
            xn[b, s * T:(s + 1) * T] = r[ci]["x_out"]
        x = xn
        if update_ctx:
            ctx = np.stack([r[b * nseg]["ctx_out"] for b in range(B)])
    return x


def kernel(**inputs):
    inp = {k: np.asarray(v) for k, v in inputs.items()}
    return _run_model(inp, inp["x"].shape[1], NSEG)
```

```python
import contextlib
import numpy as np
import concourse.bass as bass
import concourse.mybir as mybir
from concourse.bass_utils import run_bass_kernel_spmd

F32 = mybir.dt.float32
BF16 = mybir.dt.bfloat16
AF = mybir.ActivationFunctionType
ALU = mybir.AluOpType

D_MODEL = 2048
BATCH = 2
SEQ = 8192
GRID_W = 64
CTX = 256
EPS = 1e-6
NSEG = 4
EVEN_IN = 14400
COL_Q, COL_GATE, COL_Z, COL_K, COL_V, COL_XBC, COL_DT = 0, 2048, 4096, 6144, 8192, 10240, 14336
NEG = -30000.0

COMPUTE = ("pe", "act", "dve", "pool")
NDMASEM = 12
ENGNAME = {"sp": "sync", "act": "scalar", "pool": "gpsimd", "dve": "vector", "pe": "tensor"}


class SemPool:
    def __init__(self, nc):
        self.nc = nc
        self.es = contextlib.ExitStack()
        self.csem = {e: self.es.enter_context(nc.semaphore("c_" + e)) for e in COMPUTE}
        self.dsem = {q: [self.es.enter_context(nc.semaphore("d_%s_%d" % (q, i))) for i in range(NDMASEM)]
                     for q in ("sp", "act")}
        self.ccsem = self.es.enter_context(nc.semaphore("cc"))
        self.ccount = 0
        self.cbase = {e: 0 for e in COMPUTE}
        self.dcount = {"sp": 0, "act": 0}

    def close(self):
        self.es.close()


def sempool(nc):
    if not hasattr(nc, "_sempool"):
        nc._sempool = SemPool(nc)
    return nc._sempool


class Prog:
    def __init__(self, nc):
        self.nc = nc
        self.sp = sempool(nc)
        self.ops = []
        self.lastw = {}
        self.readers = {}
        self.dma_count = dict(self.sp.dcount)

    def _add(self, eng, kind, fn, reads, writes):
        idx = len(self.ops)
        deps = set()
        for k in reads:
            w = self.lastw.get(k)
            if w is not None:
                deps.add(w)
        for k in writes:
            w = self.lastw.get(k)
            if w is not None:
                deps.add(w)
            for r in self.readers.get(k, ()):
                deps.add(r)
        op = dict(eng=eng, kind=kind, fn=fn, deps=deps, idx=idx, sig=False)
        if kind == "d":
            op["dn"] = self.dma_count[eng]
            self.dma_count[eng] += 1
        self.ops.append(op)
        for k in reads:
            self.readers.setdefault(k, []).append(idx)
        for k in writes:
            self.lastw[k] = idx
            self.readers[k] = []
        return idx

    def op(self, eng, fn, reads=(), writes=()):
        return self._add(eng, "c", fn, tuple(reads), tuple(writes))

    def dma(self, eng, out, in_, reads=(), writes=(), **kw):
        return self._add(eng, "d", lambda e: e.dma_start(out=out, in_=in_, **kw), tuple(reads), tuple(writes))

    def allgather(self, out, in_, reads=(), writes=()):
        idx = self._add("pool", "x", lambda e: e.collective_compute(
            "AllGather", ALU.bypass, replica_groups=[list(range(8))], ins=[in_.opt()], outs=[out.opt()]),
            tuple(reads) + ("__cc",), tuple(writes) + ("__cc",))
        self.sp.ccount += 1
        self.ops[idx]["xn"] = self.sp.ccount
        return idx

    def emit(self):
        nc, ops, sp = self.nc, self.ops, self.sp
        csem, dsem = sp.csem, sp.dsem
        streams = {e: [] for e in ("pe", "act", "dve", "pool", "sp")}
        for o in ops:
            streams[o["eng"]].append(o)

        def skip(t, o):
            return t["kind"] == "c" and o["kind"] == "c" and t["eng"] == "pe" and o["eng"] == "pe"

        for o in ops:
            for d in o["deps"]:
                t = ops[d]
                if t["kind"] == "c" and not skip(t, o):
                    t["sig"] = True
        for e in COMPUTE:
            c = sp.cbase[e]
            for o in streams[e]:
                if o["kind"] == "c" and o["sig"]:
                    c += 1
                    o["sigval"] = c
            sp.cbase[e] = c
        for e, st in streams.items():
            waited = {}
            for o in st:
                waits = {}
                for d in o["deps"]:
                    t = ops[d]
                    if t["kind"] == "c":
                        if skip(t, o):
                            continue
                        key, val = ("c", t["eng"]), t["sigval"]
                    elif t["kind"] == "x":
                        key, val = ("x",), t["xn"]
                    else:
                        key, val = ("d", t["eng"], t["dn"] % NDMASEM), 16 * (t["dn"] // NDMASEM + 1)
                    if val > waits.get(key, 0):
                        waits[key] = val
                if o["kind"] == "d" and o["dn"] >= NDMASEM:
                    key, val = ("d", e, o["dn"] % NDMASEM), 16 * (o["dn"] // NDMASEM)
                    if val > waits.get(key, 0):
                        waits[key] = val
                fw = []
                for key, val in waits.items():
                    if waited.get(key, 0) >= val:
                        continue
                    waited[key] = val
                    fw.append((key, val))
                o["waits"] = fw
        with nc.Block() as block:
            def make(e):
                def body(eng):
                    for o in streams[e]:
                        for key, val in o["waits"]:
                            sem = csem[key[1]] if key[0] == "c" else sp.ccsem if key[0] == "x" else dsem[key[1]][key[2]]
                            eng.wait_ge(sem, val)
                        ins = o["fn"](eng)
                        if o["kind"] == "d":
                            ins.then_inc(dsem[e][o["dn"] % NDMASEM], 16)
                        elif o["kind"] == "x":
                            ins.then_inc(sp.ccsem, 1)
                        elif o["sig"]:
                            ins.then_inc(csem[e], 1)
                    if e in ("sp", "act"):
                        n = self.dma_count[e]
                        for s in range(min(n, NDMASEM)):
                            eng.wait_ge(dsem[e][s], 16 * ((n - s + NDMASEM - 1) // NDMASEM))
                return body
            for e in ("sp", "act", "pool", "dve", "pe"):
                if streams[e] or e in ("sp", "act"):
                    getattr(block, ENGNAME[e])(make(e))
        sp.dcount = dict(self.dma_count)


class Phase:
    _cnt = 0

    def __init__(self, nc, name):
        self.nc, self.name = nc, name
        self.es = contextlib.ExitStack()
        self.P = Prog(nc)
        self.q = 0

    def __enter__(self):
        return self

    def __exit__(self, et, ev, tb):
        if et is None:
            self.P.emit()
        self.es.close()
        return False

    def _nm(self, name):
        Phase._cnt += 1
        return "%s_%s_%d" % (self.name, name, Phase._cnt)

    def sb(self, shape, dt, name="t"):
        return self.es.enter_context(self.nc.sbuf_tensor(self._nm(name), list(shape), dt))

    def ps(self, shape=(128, 512), dt=F32, name="p"):
        return self.es.enter_context(self.nc.psum_tensor(self._nm(name), list(shape), dt))

    def ldq(self):
        self.q ^= 1
        return "sp"


def phase_mod(nc, c2T, ada_w, ada_b, norm_g, modD, tag):
    with Phase(nc, "mod" + tag) as ph:
        P = ph.P
        c2 = ph.sb([128, 16, 2], F32)
        sil = ph.sb([128, 16, 2], F32)
        mod = ph.sb([2, 6144], F32)
        bia = ph.sb([2, 6144], F32)
        gn = ph.sb([2, 2048], F32)
        wt = [ph.sb([128, 16, 512], F32, "wt") for _ in range(2)]
        pm = [ph.ps() for _ in range(2)]
        P.dma("sp", c2[:], c2T, writes=["c2"])
        P.dma("sp", bia[:], ada_b.partition_broadcast(2), writes=["bia"])
        P.dma("sp", gn[:], norm_g.partition_broadcast(2), writes=["gn"])
        P.op("act", lambda e: e.activation(sil[:], c2[:], AF.Silu), reads=["c2"], writes=["sil"])
        for cb in range(12):
            b = cb % 2
            src = ada_w[:, cb * 512:(cb + 1) * 512].rearrange("(c p) n -> p c n", p=128)
            for s in range(4):
                P.dma("sp", wt[b][:, 4 * s:4 * s + 4, :], src[:, 4 * s:4 * s + 4, :], writes=[("wt", b, s)])
            for kc in range(16):
                P.op("pe", lambda e, b=b, kc=kc: e.matmul(pm[b][0:2, :], sil[:, kc, :], wt[b][:, kc, :],
                                                          start=(kc == 0), stop=(kc == 15)),
                     reads=["sil", ("wt", b, kc // 4)], writes=[("pm", b)])
            P.op("dve", lambda e, b=b, cb=cb: e.tensor_tensor(mod[:, cb * 512:(cb + 1) * 512], pm[b][0:2, :],
                                                              bia[:, cb * 512:(cb + 1) * 512], ALU.add),
                 reads=[("pm", b), "bia"], writes=["mod"])
        P.op("dve", lambda e: e.scalar_tensor_tensor(mod[:, 2048:4096], mod[:, 2048:4096], 1.0, gn[:], ALU.add, ALU.mult),
             reads=["mod", "gn"], writes=["mod"])
        P.dma("sp", modD, mod[:], reads=["mod"], writes=["modD"])


def phase_norm(nc, segs, modD, HT, ident_d, tag):
    with Phase(nc, "nrm" + tag) as ph:
        P = ph.P
        rows = sorted(set(r for _, r, _ in segs))
        G = {r: ph.sb([128, 2048], F32, "G") for r in rows}
        S = {r: ph.sb([128, 2048], F32, "S") for r in rows}
        ident = ph.sb([128, 128], BF16, "id")
        P.dma("sp", ident[:], ident_d, writes=["ident"])
        for r in rows:
            P.dma("sp", G[r][:], modD[r, 2048:4096].partition_broadcast(128), writes=[("G", r)])
            P.dma("sp", S[r][:], modD[r, 0:2048].partition_broadcast(128), writes=[("S", r)])
        xt = [ph.sb([128, 2048], F32, "x") for _ in range(2)]
        t1 = [ph.sb([128, 2048], F32, "t1") for _ in range(2)]
        hb = [ph.sb([128, 2048], BF16, "hb") for _ in range(2)]
        ho = [ph.sb([128, 16, 128], BF16, "ho") for _ in range(2)]
        st = [ph.sb([128, 4], F32, "st") for _ in range(2)]
        junk = [ph.sb([128, 2048], BF16, "junk") for _ in range(2)]
        pT = [ph.ps([128, 8, 128], BF16) for _ in range(2)]
        it = 0
        tok0 = 0
        for src, r, n in segs:
            for ti in range(n // 128):
                b = it % 2
                P.dma("sp", xt[b][:], src[ti * 128:(ti + 1) * 128, :], writes=[("x", b)])
                P.op("act", lambda e, b=b: e.activation(junk[b][:], xt[b][:], AF.Square, accum_out=st[b][:, 0:1]),
                     reads=[("x", b)], writes=[("junk", b), ("st0", b)])
                P.op("act", lambda e, b=b: e.activation(st[b][:, 1:2], st[b][:, 0:1], AF.Sqrt, scale=1.0 / D_MODEL, bias=EPS),
                     reads=[("st0", b)], writes=[("st1", b)])
                P.op("dve", lambda e, b=b: e.reciprocal(st[b][:, 2:3], st[b][:, 1:2]), reads=[("st1", b)], writes=[("st2", b)])
                P.op("dve", lambda e, b=b, r=r: e.scalar_tensor_tensor(t1[b][:], xt[b][:], st[b][:, 2:3], G[r][:], ALU.mult, ALU.mult),
                     reads=[("x", b), ("st2", b), ("G", r)], writes=[("t1", b)])
                P.op("pool", lambda e, b=b, r=r: e.tensor_tensor(hb[b][:], t1[b][:], S[r][:], ALU.add),
                     reads=[("t1", b), ("S", r)], writes=[("hb", b)])
                for half in range(2):
                    for j in range(8):
                        kc = half * 8 + j
                        P.op("pe", lambda e, b=b, kc=kc, half=half, j=j: e.transpose(pT[half][:, j, :], hb[b][:, kc * 128:(kc + 1) * 128], ident[:]),
                             reads=[("hb", b), "ident"], writes=[("pT", half)])
                    eng = "act" if half == 0 else "dve"
                    if eng == "act":
                        P.op("act", lambda e, b=b, half=half: e.copy(ho[b][:, half * 8:half * 8 + 8, :], pT[half][:]),
                             reads=[("pT", half)], writes=[("ho", b, half)])
                    else:
                        P.op("dve", lambda e, b=b, half=half: e.tensor_copy(ho[b][:, half * 8:half * 8 + 8, :], pT[half][:]),
                             reads=[("pT", half)], writes=[("ho", b, half)])
                t = tok0 + ti * 128
                P.dma("act", HT[:, t:t + 128].rearrange("(c p) n -> p c n", p=128), ho[b][:],
                      reads=[("ho", b, 0), ("ho", b, 1)], writes=["HT"])
                it += 1
            tok0 += n


def phase_proj(nc, HT, NT, fams, tag):
    with Phase(nc, "prj" + tag) as ph:
        P = ph.P
        hT = ph.sb([128, 16, NT], BF16, "hT")
        src = HT.rearrange("(c p) n -> p c n", p=128)
        for c in range(16):
            P.dma("sp", hT[:, c, :], src[:, c, :], writes=["hT"])
        wf = [ph.sb([128, 16, 256], F32, "wf") for _ in range(2)]
        wb = [ph.sb([128, 16, 256], BF16, "wb") for _ in range(2)]
        acc = [ph.ps() for _ in range(3)]
        nps = [ph.ps() for _ in range(2)]
        of = [ph.sb([128, 512], F32, "of") for _ in range(2)]
        ob = [ph.sb([128, 512], BF16, "ob") for _ in range(2)]
        sq = [ph.sb([128, 512], BF16, "sq") for _ in range(2)]
        rs = [ph.sb([128, 512], F32, "rs") for _ in range(2)]
        onesb = ph.sb([128, 128], BF16, "ones")
        P.op("pool", lambda e: e.memset(onesb[:], 1.0), writes=["ones"])
        gains = {}
        for fi, f in enumerate(fams):
            if f.get("norm") is not None:
                g = ph.sb([128, 1], F32, "gain")
                P.dma("sp", g[:], f["norm"], writes=[("gain", fi)])
                gains[fi] = g
        tblocks = [(t, min(512, NT - t)) for t in range(0, NT, 512)]
        wi = 0
        ai = 0
        oi = 0
        ni = 0
        for fi, f in enumerate(fams):
            W, col0, ncols, out, odt = f["W"], f["col0"], f["ncols"], f["out"], f["odt"]
            fblocks = f.get("blocks", tblocks)
            ftiles = f.get("tiles", list(range(NT // 128)))
            for g0 in range(0, ncols, 256):
                gw = min(256, ncols - g0)
                b = wi % 2
                wi += 1
                wsrc = W[:, col0 + g0:col0 + g0 + gw].rearrange("(c p) n -> p c n", p=128)
                for s in range(2):
                    P.dma("sp", wf[b][:, 8 * s:8 * s + 8, 0:gw], wsrc[:, 8 * s:8 * s + 8, :], writes=[("wf", b, s)])
                    P.op("pool", lambda e, b=b, s=s, gw=gw: e.tensor_copy(wb[b][:, 8 * s:8 * s + 8, 0:gw], wf[b][:, 8 * s:8 * s + 8, 0:gw]),
                         reads=[("wf", b, s)], writes=[("wb", b, s)])
                if f["mode"] == "fm":
                    for c0 in range(0, gw, 128):
                        ch = (g0 + c0) // 128
                        for (t0, tn) in fblocks:
                            a = ai % 3
                            ai += 1
                            for kc in range(16):
                                P.op("pe", lambda e, a=a, b=b, kc=kc, c0=c0, t0=t0, tn=tn: e.matmul(
                                    acc[a][:, 0:tn], wb[b][:, kc, c0:c0 + 128], hT[:, kc, t0:t0 + tn], start=(kc == 0), stop=(kc == 15)),
                                    reads=[("wb", b, kc // 8), "hT"], writes=[("acc", a)])
                            o = oi % 2
                            oi += 1
                            ot = of[o] if odt == F32 else ob[o]
                            okey = ("of", o) if odt == F32 else ("ob", o)
                            if fi in gains:
                                n = ni % 2
                                ni += 1
                                P.op("act", lambda e, a=a, n=n, tn=tn: e.activation(sq[n][:, 0:tn], acc[a][:, 0:tn], AF.Square),
                                     reads=[("acc", a)], writes=[("sq", n)])
                                P.op("pe", lambda e, n=n, tn=tn: e.matmul(nps[n][:, 0:tn], onesb[:], sq[n][:, 0:tn], start=True, stop=True),
                                     reads=[("sq", n), "ones"], writes=[("nps", n)])
                                P.op("act", lambda e, n=n, tn=tn: e.activation(rs[n][:, 0:tn], nps[n][:, 0:tn], AF.Sqrt, scale=1.0 / 128, bias=EPS),
                                     reads=[("nps", n)], writes=[("rs", n)])
                                P.op("dve", lambda e, n=n, tn=tn: e.reciprocal(rs[n][:, 0:tn], rs[n][:, 0:tn]),
                                     reads=[("rs", n)], writes=[("rs", n)])
                                P.op("dve", lambda e, a=a, n=n, tn=tn, ot=ot, fi=fi: e.scalar_tensor_tensor(
                                    ot[:, 0:tn], acc[a][:, 0:tn], gains[fi][:, 0:1], rs[n][:, 0:tn], ALU.mult, ALU.mult),
                                    reads=[("acc", a), ("rs", n), ("gain", fi)], writes=[okey])
                            else:
                                P.op("act", lambda e, a=a, tn=tn, ot=ot: e.copy(ot[:, 0:tn], acc[a][:, 0:tn]),
                                     reads=[("acc", a)], writes=[okey])
                            P.dma("act", out[ch * 128:(ch + 1) * 128, t0:t0 + tn], ot[:, 0:tn], reads=[okey], writes=[("out", fi)])
                else:
                    for tt in ftiles:
                        a = ai % 3
                        ai += 1
                        for kc in range(16):
                            P.op("pe", lambda e, a=a, b=b, kc=kc, tt=tt, gw=gw: e.matmul(
                                acc[a][:, 0:gw], hT[:, kc, tt * 128:(tt + 1) * 128], wb[b][:, kc, 0:gw], start=(kc == 0), stop=(kc == 15)),
                                reads=[("wb", b, kc // 8), "hT"], writes=[("acc", a)])
                        o = oi % 2
                        oi += 1
                        ot = of[o] if odt == F32 else ob[o]
                        okey = ("of", o) if odt == F32 else ("ob", o)
                        P.op("act", lambda e, a=a, gw=gw, ot=ot: e.copy(ot[:, 0:gw], acc[a][:, 0:gw]),
                             reads=[("acc", a)], writes=[okey])
                        P.dma("act", out[tt * 128:(tt + 1) * 128, g0:g0 + gw], ot[:, 0:gw], reads=[okey], writes=[("out", fi)])


def phase_outproj(nc, YT, KC, W_out, segs, modD, tag):
    with Phase(nc, "out" + tag) as ph:
        P = ph.P
        rows = sorted(set(s[2] for s in segs))
        gt = {r: ph.sb([128, 2048], F32, "gate") for r in rows}
        for r in rows:
            P.dma("sp", gt[r][:], modD[r, 4096:6144].partition_broadcast(128), writes=[("gate", r)])
        wf = [ph.sb([128, 4, 512], F32, "wf") for _ in range(2)]
        wb = [ph.sb([128, KC, 512], BF16, "wb") for _ in range(2)]
        yt = [ph.sb([128, KC, 512], BF16, "yt") for _ in range(2)]
        xt = [ph.sb([128, 512], F32, "xt") for _ in range(3)]
        t1 = [ph.sb([128, 512], F32, "t1") for _ in range(2)]
        ot = [ph.sb([128, 512], F32, "ot") for _ in range(2)]
        acc = [ph.ps() for _ in range(3)]
        ysrc = YT.rearrange("(c p) n -> p c n", p=128)
        wi = 0
        it = 0
        gi = 0
        for db in range(4):
            b = db % 2
            wsrc = W_out[:, db * 512:(db + 1) * 512].rearrange("(c p) n -> p c n", p=128)
            for s in range(KC // 4):
                f = wi % 2
                wi += 1
                P.dma("sp", wf[f][:], wsrc[:, 4 * s:4 * s + 4, :], writes=[("wf", f)])
                P.op("pool", lambda e, f=f, b=b, s=s: e.tensor_copy(wb[b][:, 4 * s:4 * s + 4, :], wf[f][:]),
                     reads=[("wf", f)], writes=[("wb", b, s // 2)])
            for (xin, xout, r, n, tok0) in segs:
                for t0 in range(0, n, 512):
                    tw = min(512, n - t0)
                    y = gi % 2
                    gi += 1
                    P.dma("sp", yt[y][:, :, 0:tw], ysrc[:, :, tok0 + t0:tok0 + t0 + tw], writes=[("yt", y)])
                    for j in range(tw // 128):
                        ti = (t0 // 128) + j
                        x3 = it % 3
                        a = it % 3
                        o = it % 2
                        it += 1
                        P.dma("sp", xt[x3][:], xin[ti * 128:(ti + 1) * 128, db * 512:(db + 1) * 512], writes=[("xt", x3)])
                        for kc in range(KC):
                            P.op("pe", lambda e, a=a, y=y, b=b, kc=kc, j=j: e.matmul(acc[a][:], yt[y][:, kc, j * 128:(j + 1) * 128], wb[b][:, kc, :],
                                                                                  start=(kc == 0), stop=(kc == KC - 1)),
                                 reads=[("yt", y), ("wb", b, kc // 8)], writes=[("acc", a)])
                        P.op("dve", lambda e, a=a, o=o, r=r, db=db: e.tensor_tensor(t1[o][:], acc[a][:], gt[r][:, db * 512:(db + 1) * 512], ALU.mult),
                             reads=[("acc", a), ("gate", r)], writes=[("t1", o)])
                        P.op("pool", lambda e, o=o, x3=x3: e.tensor_tensor(ot[o][:], t1[o][:], xt[x3][:], ALU.add),
                             reads=[("t1", o), ("xt", x3)], writes=[("ot", o)])
                        P.dma("act", xout[ti * 128:(ti + 1) * 128, db * 512:(db + 1) * 512], ot[o][:], reads=[("ot", o)], writes=["xout"])


def phase_convgate(nc, PT, NTA, seqs, cwT, flags, YT, tag):
    with Phase(nc, "cg" + tag) as ph:
        P = ph.P
        cw = ph.sb([128, 16, 3], F32, "cw")
        fl = ph.sb([128, 2], F32, "fl")
        P.dma("sp", cw[:], cwT, writes=["cw"])
        P.dma("sp", fl[:], flags, writes=["fl"])
        inb = [[ph.sb([128, NTA], F32, "in") for _ in range(4)] for _ in range(2)]
        nmax = max(n for _, n, _ in seqs)
        u = [ph.sb([128, nmax + 2], F32, "u") for _ in range(2)]
        v = [ph.sb([128, nmax], F32, "v") for _ in range(2)]
        sg = [ph.sb([128, nmax], F32, "sg") for _ in range(2)]
        yo = [ph.sb([128, nmax], BF16, "yo") for _ in range(2)]
        it = 0
        for c in range(16):
            b = c % 2
            for k in range(4):
                P.dma("sp", inb[b][k][:], PT[k * 2048 + c * 128:k * 2048 + (c + 1) * 128, :], writes=[("in", b, k)])
            bg, cg, hv, g = inb[b]
            for (tok0, n, halo) in seqs:
                i = it % 2
                it += 1
                rd = [("in", b, 1), ("in", b, 2)]
                P.op("pool", lambda e, i=i, n=n, tok0=tok0, cg=cg, hv=hv: e.tensor_tensor(u[i][:, 1:n + 1], cg[:, tok0:tok0 + n], hv[:, tok0:tok0 + n], ALU.mult),
                     reads=rd, writes=[("u", i)])
                if halo is None:
                    P.op("pool", lambda e, i=i: e.memset(u[i][:, 0:1], 0.0), writes=[("ul", i)])
                    P.op("pool", lambda e, i=i, n=n: e.memset(u[i][:, n + 1:n + 2], 0.0), writes=[("ur", i)])
                else:
                    tb, ta = halo
                    P.op("dve", lambda e, i=i, tb=tb, cg=cg, hv=hv: e.scalar_tensor_tensor(u[i][:, 0:1], cg[:, tb:tb + 1], fl[:, 0:1], hv[:, tb:tb + 1], ALU.mult, ALU.mult),
                         reads=rd + ["fl"], writes=[("ul", i)])
                    P.op("dve", lambda e, i=i, ta=ta, n=n, cg=cg, hv=hv: e.scalar_tensor_tensor(u[i][:, n + 1:n + 2], cg[:, ta:ta + 1], fl[:, 1:2], hv[:, ta:ta + 1], ALU.mult, ALU.mult),
                         reads=rd + ["fl"], writes=[("ur", i)])
                uk = [("u", i), ("ul", i), ("ur", i)]
                P.op("dve", lambda e, i=i, n=n, c=c: e.tensor_scalar(v[i][:, 0:n], u[i][:, 0:n], cw[:, c, 0:1], None, ALU.mult),
                     reads=uk + ["cw"], writes=[("v", i)])
                P.op("dve", lambda e, i=i, n=n, c=c: e.scalar_tensor_tensor(v[i][:, 0:n], u[i][:, 1:n + 1], cw[:, c, 1:2], v[i][:, 0:n], ALU.mult, ALU.add),
                     reads=uk + ["cw", ("v", i)], writes=[("v", i)])
                P.op("dve", lambda e, i=i, n=n, c=c: e.scalar_tensor_tensor(v[i][:, 0:n], u[i][:, 2:n + 2], cw[:, c, 2:3], v[i][:, 0:n], ALU.mult, ALU.add),
                     reads=uk + ["cw", ("v", i)], writes=[("v", i)])
                P.op("act", lambda e, i=i, n=n, tok0=tok0, g=g: e.activation(sg[i][:, 0:n], g[:, tok0:tok0 + n], AF.Silu),
                     reads=[("in", b, 3)], writes=[("sg", i)])
                P.op("pool", lambda e, i=i, n=n, tok0=tok0, bg=bg: e.tensor_tensor(sg[i][:, 0:n], sg[i][:, 0:n], bg[:, tok0:tok0 + n], ALU.mult),
                     reads=[("sg", i), ("in", b, 0)], writes=[("sg", i)])
                P.op("dve", lambda e, i=i, n=n: e.tensor_tensor(yo[i][:, 0:n], v[i][:, 0:n], sg[i][:, 0:n], ALU.mult),
                     reads=[("v", i), ("sg", i)], writes=[("yo", i)])
                P.dma("act", YT[c * 128:(c + 1) * 128, tok0:tok0 + n], yo[i][:, 0:n], reads=[("yo", i)], writes=["YT"])


def _dt(nc, n, s, d=F32, k="Internal"):
    return nc.dram_tensor(n, list(s), d, kind=k).ap()


def build_odd(T, with_ctx):
    nc = bass.Bass("TRN2", target_bir_lowering=False)
    I = lambda n, s, d=F32: _dt(nc, n, s, d, "ExternalInput")
    O = lambda n, s, d=F32: _dt(nc, n, s, d, "ExternalOutput")
    xh = I("xh", [T + 128, 2048])
    c2T = I("c2T", [128, 16, 2]); ada_w = I("ada_w", [2048, 6144]); ada_b = I("ada_b", [6144]); norm_g = I("norm_g", [2048])
    w_in = I("w_in", [2048, 8192]); cwT = I("cwT", [128, 16, 3]); flags = I("flags", [128, 2]); w_out = I("w_out", [2048, 2048])
    ident = I("ident", [128, 128], BF16)
    x_out = O("x_out", [T, 2048])
    NTA = T + 128 + (CTX if with_ctx else 0)
    if with_ctx:
        ctx = I("ctx", [CTX, 2048]); ctx_out = O("ctx_out", [CTX, 2048])
    modD = _dt(nc, "modD", [2, 6144]); HT = _dt(nc, "HT", [2048, NTA], BF16)
    PT = _dt(nc, "PT", [8192, NTA]); YT = _dt(nc, "YT", [2048, NTA], BF16)
    phase_mod(nc, c2T, ada_w, ada_b, norm_g, modD, "o")
    segs = [(xh, 0, T + 128)] + ([(ctx, 1, CTX)] if with_ctx else [])
    phase_norm(nc, segs, modD, HT, ident, "o")
    phase_proj(nc, HT, NTA, [dict(mode="fm", W=w_in, col0=0, ncols=8192, out=PT, odt=F32, norm=None)], "o")
    seqs = [(0, T, (T, T + 1))] + ([(T + 128, CTX, None)] if with_ctx else [])
    phase_convgate(nc, PT, NTA, seqs, cwT, flags, YT, "o")
    osegs = [(xh, x_out, 0, T, 0)] + ([(ctx, ctx_out, 1, CTX, T + 128)] if with_ctx else [])
    phase_outproj(nc, YT, 16, w_out, osegs, modD, "o")
    sempool(nc).close()
    return nc


def phase_convsilu(nc, PX, T, cw5T, cb5T, flags, ident_d, XCT, XS, BM, tag):
    NTL = T + CTX
    W = T + 4 + CTX + 4
    with Phase(nc, "cs" + tag) as ph:
        P = ph.P
        cw = ph.sb([128, 32, 5], F32, "cw")
        cb = ph.sb([128, 32], F32, "cb")
        fl = ph.sb([128, 2], F32, "fl")
        ident = ph.sb([128, 128], BF16, "id")
        P.dma("sp", cw[:], cw5T, writes=["cw"])
        P.dma("sp", cb[:], cb5T, writes=["cb"])
        P.dma("sp", fl[:], flags, writes=["fl"])
        P.dma("sp", ident[:], ident_d, writes=["ident"])
        u = [ph.sb([128, W], F32, "u") for _ in range(2)]
        acc = [ph.sb([128, NTL], F32, "acc") for _ in range(2)]
        xc = [ph.sb([128, NTL], BF16, "xc") for _ in range(2)]
        tb = [ph.sb([128, NTL // 128, 128], BF16, "tb") for _ in range(2)]
        pT = [ph.ps([128, 8, 128], BF16) for _ in range(2)]
        c0 = T + 4
        for b in range(2):
            P.op("pool", lambda e, b=b: e.memset(u[b][:, c0:c0 + 2], 0.0), writes=[("uz", b)])
            P.op("pool", lambda e, b=b: e.memset(u[b][:, c0 + 2 + CTX:c0 + 4 + CTX], 0.0), writes=[("uz", b)])
        pi = 0
        for c in range(32):
            b = c % 2
            row = PX[c * 128:(c + 1) * 128, :]
            P.dma("sp", u[b][:, 2:T + 2], row[:, 0:T], writes=[("u", b)])
            P.dma("sp", u[b][:, 0:2], row[:, T + 254:T + 256], writes=[("ul", b)])
            P.dma("sp", u[b][:, T + 2:T + 4], row[:, T + 256:T + 258], writes=[("ur", b)])
            P.dma("sp", u[b][:, c0 + 2:c0 + 2 + CTX], row[:, T + 512:T + 512 + CTX], writes=[("uc", b)])
            P.op("dve", lambda e, b=b: e.tensor_scalar(u[b][:, 0:2], u[b][:, 0:2], fl[:, 0:1], None, ALU.mult),
                 reads=[("ul", b), "fl"], writes=[("ul", b)])
            P.op("dve", lambda e, b=b: e.tensor_scalar(u[b][:, T + 2:T + 4], u[b][:, T + 2:T + 4], fl[:, 1:2], None, ALU.mult),
                 reads=[("ur", b), "fl"], writes=[("ur", b)])
            for (o0, n, a0) in ((0, T, 0), (c0, CTX, T)):
                rd = [("u", b), ("ul", b), ("ur", b), ("uc", b), ("uz", b), "cw"]
                P.op("dve", lambda e, b=b, c=c, o0=o0, n=n, a0=a0: e.tensor_scalar(acc[b][:, a0:a0 + n], u[b][:, o0:o0 + n], cw[:, c, 0:1], None, ALU.mult),
                     reads=rd, writes=[("acc", b, a0)])
                for k in range(1, 5):
                    P.op("dve", lambda e, b=b, c=c, o0=o0, n=n, a0=a0, k=k: e.scalar_tensor_tensor(
                        acc[b][:, a0:a0 + n], u[b][:, o0 + k:o0 + k + n], cw[:, c, k:k + 1], acc[b][:, a0:a0 + n], ALU.mult, ALU.add),
                        reads=rd + [("acc", b, a0)], writes=[("acc", b, a0)])
            P.op("act", lambda e, b=b, c=c: e.activation(xc[b][:], acc[b][:], AF.Silu, bias=cb[:, c:c + 1]),
                 reads=[("acc", b, 0), ("acc", b, T), "cb"], writes=[("xc", b)])
            P.dma("act", XCT[c * 128:(c + 1) * 128, :], xc[b][:], reads=[("xc", b)], writes=["XCT"])
            if c < 24:
                nt = NTL // 128
                for t0 in range(0, nt, 8):
                    p = pi % 2
                    pi += 1
                    tn = min(8, nt - t0)
                    for j in range(tn):
                        P.op("pe", lambda e, b=b, p=p, j=j, t0=t0: e.transpose(pT[p][:, j, :], xc[b][:, (t0 + j) * 128:(t0 + j + 1) * 128], ident[:]),
                             reads=[("xc", b), "ident"], writes=[("pT", p)])
                    P.op("act", lambda e, b=b, p=p, t0=t0, tn=tn: e.copy(tb[b][:, t0:t0 + tn, :], pT[p][:, 0:tn, :]),
                         reads=[("pT", p)], writes=[("tb", b)])
                dst = XS[:, c * 128:(c + 1) * 128] if c < 16 else BM[:, (c - 16) * 128:(c - 15) * 128]
                P.dma("act", dst.rearrange("(t p) n -> p t n", p=128), tb[b][:], reads=[("tb", b)], writes=["XSBM"])


def phase_ssd(nc, T, XS, BM, XCT, DTR, consts, a_log, dt_bias, tag, mode,
              Sseg=None, Dseg=None, Sall=None, Dall=None, mk=None, YF=None, YB=None, nseg=NSEG):
    with Phase(nc, "ssd" + tag) as ph:
        P = ph.P
        cst_ = ph.sb([128, 4, 128], F32, "consts")
        P.dma("sp", cst_[:], consts.rearrange("k p n -> p k n"), writes=["consts"])
        tri = [cst_[:, 0, :], cst_[:, 1, :]]
        Ls = [cst_[:, 2, :], cst_[:, 3, :]]
        ones = ph.sb([128, 128], F32, "ones")
        P.op("pool", lambda e: e.memset(ones[:], 1.0), writes=["ones"])
        aB = ph.sb([128, 64], F32, "aB")
        dtbB = ph.sb([128, 64], F32, "dtbB")
        P.dma("sp", aB[:], a_log.partition_broadcast(128), writes=["aB"])
        P.dma("sp", dtbB[:], dt_bias.partition_broadcast(128), writes=["dtbB"])
        P.op("act", lambda e: e.activation(aB[:], aB[:], AF.Exp), reads=["aB"], writes=["aB"])
        P.op("dve", lambda e: e.tensor_scalar(aB[:], aB[:], -1.0, None, ALU.mult), reads=["aB"], writes=["aB"])
        hT = [ph.sb([128, 8, 256], F32, "hT") for _ in range(2)]
        hTb = [ph.sb([128, 8, 256], BF16, "hTb") for _ in range(2)]
        dlog = [ph.sb([128, 32], F32, "dlog") for _ in range(2)]
        for d in range(2):
            P.op("pool", lambda e, d=d: e.memset(hT[d][:], 0.0), writes=[("hT", d, g) for g in range(8)])
            P.op("pool", lambda e, d=d: e.memset(hTb[d][:], 0.0), writes=[("hTb", d, g) for g in range(8)])
            P.op("pool", lambda e, d=d: e.memset(dlog[d][:], 0.0), writes=[("dlog", d)])
        xs = [ph.sb([128, 2048], BF16, "xs") for _ in range(2)]
        bm = [ph.sb([128, 1024], BF16, "bm") for _ in range(2)]
        bt = [ph.sb([128, 8, 128], BF16, "bt") for _ in range(2)]
        ct = [ph.sb([128, 8, 128], BF16, "ct") for _ in range(2)]
        dtr = [ph.sb([128, 32], F32, "dtr") for _ in range(2)]
        dtx = [ph.sb([128, 32], F32, "dtx") for _ in range(2)]
        dt = [ph.sb([128, 32], F32, "dt") for _ in range(2)]
        dta = [ph.sb([128, 32], F32, "dta") for _ in range(2)]
        cst = [ph.sb([128, 64], F32, "cst") for _ in range(2)]
        ecd = [ph.sb([128, 64], F32, "ecd") for _ in range(2)]
        tm = [ph.sb([128, 32], F32, "tm") for _ in range(2)]
        te = [ph.sb([128, 32], F32, "te") for _ in range(2)]
        yt = [ph.sb([128, 2048], F32, "yt") for _ in range(2)]
        cbm = [ph.sb([128, 128], F32, "cbm") for _ in range(2)]
        rseg = [ph.sb([128, 4, 128], F32, "rseg") for _ in range(2)]
        E = [ph.sb([128, 4, 128], F32, "E") for _ in range(2)]
        MT = [ph.sb([128, 4, 128], BF16, "MT") for _ in range(2)]
        xdt = [ph.sb([128, 4, 64], BF16, "xdt") for _ in range(2)]
        xw = [ph.sb([128, 4, 64], BF16, "xw") for _ in range(2)]
        ytmp = [ph.sb([128, 4, 64], F32, "ytmp") for _ in range(2)]
        ps_s = ph.ps()
        ps_cb = ph.ps()
        ps_seg = [ph.ps() for _ in range(2)]
        ps_y = ph.ps()
        ps_yi = ph.ps()
        ps_st2 = [ph.ps() for _ in range(2)]
        cnt = {"it": 0, "g": 0}
        Y = [YF, YB]

        def v4(ap):
            return ap.rearrange("p (r q) -> p r q", r=4)

        def chunk(tile, drow, d, full):
            b = cnt["it"] % 2
            cnt["it"] += 1
            tok = tile * 128
            P.dma("sp", xs[b][:], XS[tok:tok + 128, :], writes=[("xs", b)])
            P.dma("sp", bm[b][:], BM[tok:tok + 128, :], writes=[("bm", b)])
            if full:
                P.dma("sp", bt[b][:], XCT[2048:3072, tok:tok + 128].rearrange("(g p) n -> p g n", p=128), writes=[("bt", b)])
                P.dma("sp", ct[b][:], XCT[3072:4096, tok:tok + 128].rearrange("(g p) n -> p g n", p=128), writes=[("ct", b)])
            P.dma("sp", dtr[b][:], DTR[drow:drow + 128, d * 32:(d + 1) * 32], writes=[("dtr", b)])
            P.op("dve", lambda e: e.tensor_tensor(dtx[b][:], dtr[b][:], dtbB[:, d * 32:(d + 1) * 32], ALU.add),
                 reads=[("dtr", b), "dtbB"], writes=[("dtx", b)])
            P.op("act", lambda e: e.activation(dtx[b][:], dtx[b][:], AF.Exp), reads=[("dtx", b)], writes=[("dtx", b)])
            P.op("act", lambda e: e.activation(dt[b][:], dtx[b][:], AF.Ln, bias=1.0), reads=[("dtx", b)], writes=[("dt", b)])
            P.op("dve", lambda e: e.tensor_tensor(dta[b][:], dt[b][:], aB[:, d * 32:(d + 1) * 32], ALU.mult),
                 reads=[("dt", b), "aB"], writes=[("dta", b)])
            P.op("pe", lambda e: e.matmul(ps_s[:, 0:32], tri[d], dta[b][:], start=True, stop=True),
                 reads=[("dta", b), "consts"], writes=["ps_s"])
            P.op("pe", lambda e: e.matmul(ps_s[:, 32:64], ones[:], dta[b][:], start=True, stop=True),
                 reads=[("dta", b), "ones"], writes=["ps_s"])
            P.op("act", lambda e: e.copy(cst[b][:], ps_s[:, 0:64]), reads=["ps_s"], writes=[("cst", b)])
            P.op("act", lambda e: e.activation(ecd[b][:], cst[b][:], AF.Exp), reads=[("cst", b)], writes=[("ecd", b)])
            P.op("dve", lambda e: e.tensor_tensor(tm[b][:], cst[b][:, 32:64], cst[b][:, 0:32], ALU.subtract),
                 reads=[("cst", b)], writes=[("tm", b)])
            P.op("act", lambda e: e.activation(tm[b][:], tm[b][:], AF.Exp), reads=[("tm", b)], writes=[("tm", b)])
            P.op("dve", lambda e: e.tensor_tensor(te[b][:], tm[b][:], dt[b][:], ALU.mult),
                 reads=[("tm", b), ("dt", b)], writes=[("te", b)])
            if mode == "pre":
                P.op("dve", lambda e: e.tensor_tensor(dlog[d][:], dlog[d][:], cst[b][:, 32:64], ALU.add),
                     reads=[("dlog", d), ("cst", b)], writes=[("dlog", d)])
            for g in range(8):
                q = cnt["g"] % 2
                cnt["g"] += 1
                gs = slice(g * 4, g * 4 + 4)
                if full:
                    P.op("pe", lambda e, g=g: e.matmul(ps_cb[:, 0:128], bt[b][:, g, :], ct[b][:, g, :], start=True, stop=True),
                         reads=[("bt", b), ("ct", b)], writes=["ps_cb"])
                    P.op("dve", lambda e, q=q: e.tensor_tensor(cbm[q][:], ps_cb[:, 0:128], tri[d], ALU.mult),
                         reads=["ps_cb", "consts"], writes=[("cbm", q)])
                    P.op("pool", lambda e, q=q, gs=gs: e.tensor_tensor(
                        rseg[q][:], tri[d].unsqueeze(1).to_broadcast([128, 4, 128]),
                        dta[b][:, gs].unsqueeze(2).to_broadcast([128, 4, 128]), ALU.mult),
                        reads=[("dta", b), "consts"], writes=[("rseg", q)])
                    P.op("pe", lambda e, q=q: e.matmul(ps_seg[q][:], Ls[d], rseg[q][:].rearrange("p r i -> p (r i)"), start=True, stop=True),
                         reads=[("rseg", q), "consts"], writes=[("ps_seg", q)])
                    P.op("act", lambda e, q=q: e.activation(E[q][:].rearrange("p r i -> p (r i)"), ps_seg[q][:], AF.Exp),
                         reads=[("ps_seg", q)], writes=[("E", q)])
                    P.op("dve", lambda e, q=q: e.tensor_tensor(MT[q][:], E[q][:], cbm[q][:].unsqueeze(1).to_broadcast([128, 4, 128]), ALU.mult),
                         reads=[("E", q), ("cbm", q)], writes=[("MT", q)])
                    P.op("pool", lambda e, q=q, g=g, gs=gs: e.tensor_tensor(
                        xdt[q][:], v4(xs[b][:, g * 256:(g + 1) * 256]), dt[b][:, gs].unsqueeze(2).to_broadcast([128, 4, 64]), ALU.mult),
                        reads=[("xs", b), ("dt", b)], writes=[("xdt", q)])
                    yo = 0
                    for r in range(4):
                        P.op("pe", lambda e, q=q, r=r, yo=yo: e.matmul(ps_y[:, yo + r * 64:yo + (r + 1) * 64], MT[q][:, r, :], xdt[q][:, r, :], start=True, stop=True),
                             reads=[("MT", q), ("xdt", q)], writes=["ps_y"])
                    P.op("pe", lambda e, q=q, g=g, yo=yo: e.matmul(ps_yi[:, yo:yo + 256], ct[b][:, g, :], hTb[d][:, g, :], start=True, stop=True),
                         reads=[("ct", b), ("hTb", d, g)], writes=["ps_yi"])
                    P.op("dve", lambda e, q=q, gs=gs, yo=yo: e.tensor_tensor(
                        ytmp[q][:], v4(ps_yi[:, yo:yo + 256]), ecd[b][:, gs].unsqueeze(2).to_broadcast([128, 4, 64]), ALU.mult),
                        reads=["ps_yi", ("ecd", b)], writes=[("ytmp", q)])
                    P.op("dve", lambda e, q=q, g=g, yo=yo: e.tensor_tensor(
                        yt[b][:, g * 256:(g + 1) * 256], ytmp[q][:].rearrange("p r q -> p (r q)"), ps_y[:, yo:yo + 256], ALU.add),
                        reads=[("ytmp", q), "ps_y"], writes=[("yt", b, g)])
                P.op("pool", lambda e, q=q, g=g, gs=gs: e.tensor_tensor(
                    xw[q][:], v4(xs[b][:, g * 256:(g + 1) * 256]), te[b][:, gs].unsqueeze(2).to_broadcast([128, 4, 64]), ALU.mult),
                    reads=[("xs", b), ("te", b)], writes=[("xw", q)])
                so = 0
                ps_st = ps_st2[q]
                P.op("pe", lambda e, q=q, g=g, so=so, ps_st=ps_st: e.matmul(ps_st[:, so:so + 256], bm[b][:, g * 128:(g + 1) * 128],
                                                               xw[q][:].rearrange("p r q -> p (r q)"), start=True, stop=True),
                     reads=[("bm", b), ("xw", q)], writes=[("ps_st", q)])
                P.op("dve", lambda e, g=g, gs=gs: e.tensor_tensor(
                    v4(hT[d][:, g, :]), v4(hT[d][:, g, :]), ecd[b][:, 32 + g * 4:32 + g * 4 + 4].unsqueeze(2).to_broadcast([128, 4, 64]), ALU.mult),
                    reads=[("hT", d, g), ("ecd", b)], writes=[("hT", d, g)])
                P.op("dve", lambda e, g=g, so=so, ps_st=ps_st: e.tensor_tensor(hT[d][:, g, :], hT[d][:, g, :], ps_st[:, so:so + 256], ALU.add),
                     reads=[("hT", d, g), ("ps_st", q)], writes=[("hT", d, g)])
                P.op("act", lambda e, g=g: e.copy(hTb[d][:, g, :], hT[d][:, g, :]), reads=[("hT", d, g)], writes=[("hTb", d, g)])
            if full:
                P.dma("act", Y[d][tok:tok + 128, :], yt[b][:], reads=[("yt", b, g) for g in range(8)], writes=["Y"])

        nl = T // 128
        lat = [(t, t * 128) for t in range(nl)]
        cx = [(nl + j, T + 512 + j * 128) for j in range(CTX // 128)]
        if mode == "pre":
            for d in range(2):
                for (tile, drow) in (lat if d == 0 else lat[::-1]):
                    chunk(tile, drow, d, False)
                P.dma("act", Sseg[d], hT[d][:].rearrange("p g q -> p (g q)"), reads=[("hT", d, g) for g in range(8)], writes=["Sseg"])
                P.dma("act", Dseg[d], dlog[d][:], reads=[("dlog", d)], writes=["Dseg"])
        else:
            mkt = ph.sb([128, 2 * nseg], F32, "mk")
            P.dma("sp", mkt[:], mk, writes=["mk"])
            St = [ph.sb([128, 2048], F32, "St") for _ in range(2)]
            dl = [ph.sb([128, 32], F32, "dl") for _ in range(2)]
            ci = 0
            for d in range(2):
                for (tile, drow) in (cx if d == 0 else cx[::-1]):
                    chunk(tile, drow, d, True)
                allk = [("hT", d, g) for g in range(8)]
                for s in (range(nseg) if d == 0 else range(nseg - 1, -1, -1)):
                    c = ci % 2
                    ci += 1
                    col = d * nseg + s
                    P.dma("sp", St[c][:], Sall[s, d], writes=[("St", c)])
                    P.dma("sp", dl[c][:], Dall[s, d], writes=[("dl", c)])
                    P.op("dve", lambda e, c=c, col=col: e.tensor_scalar(dl[c][:], dl[c][:], mkt[:, col:col + 1], None, ALU.mult),
                         reads=[("dl", c), "mk"], writes=[("dl", c)])
                    P.op("act", lambda e, c=c: e.activation(dl[c][:], dl[c][:], AF.Exp), reads=[("dl", c)], writes=[("dl", c)])
                    P.op("dve", lambda e, c=c, d=d: e.tensor_tensor(
                        hT[d][:].rearrange("p g (r q) -> p (g r) q", r=4), hT[d][:].rearrange("p g (r q) -> p (g r) q", r=4),
                        dl[c][:].unsqueeze(2).to_broadcast([128, 32, 64]), ALU.mult),
                        reads=allk + [("dl", c)], writes=allk)
                    P.op("dve", lambda e, c=c, d=d, col=col: e.scalar_tensor_tensor(
                        hT[d][:].rearrange("p g q -> p (g q)"), St[c][:], mkt[:, col:col + 1], hT[d][:].rearrange("p g q -> p (g q)"), ALU.mult, ALU.add),
                        reads=allk + [("St", c), "mk"], writes=allk)
                P.op("act", lambda e, d=d: e.copy(hTb[d][:], hT[d][:]), reads=allk, writes=[("hTb", d, g) for g in range(8)])
                for (tile, drow) in (lat if d == 0 else lat[::-1]):
                    chunk(tile, drow, d, True)


def phase_gnorm(nc, T, YF, YB, XS, Z, d_skip, ssm_g, ident_d, YT, tag):
    NTL = T + CTX
    with Phase(nc, "gn" + tag) as ph:
        P = ph.P
        dB = ph.sb([128, 32], F32, "dB")
        gB = ph.sb([128, 2048], F32, "gB")
        ident = ph.sb([128, 128], BF16, "id")
        P.dma("sp", dB[:], d_skip.partition_broadcast(128), writes=["dB"])
        P.dma("sp", gB[:], ssm_g.partition_broadcast(128), writes=["gB"])
        P.dma("sp", ident[:], ident_d, writes=["ident"])
        yf = [ph.sb([128, 2048], F32, "yf") for _ in range(2)]
        yb = [ph.sb([128, 2048], F32, "yb") for _ in range(2)]
        xs = [ph.sb([128, 2048], BF16, "xs") for _ in range(2)]
        z = [ph.sb([128, 2048], F32, "z") for _ in range(2)]
        t1 = [ph.sb([128, 2048], F32, "t1") for _ in range(2)]
        yn = [ph.sb([128, 2048], BF16, "yn") for _ in range(2)]
        junk = [ph.sb([128, 256], BF16, "junk") for _ in range(2)]
        ss = [ph.sb([128, 16], F32, "ss") for _ in range(2)]
        ho = [ph.sb([128, 16, 128], BF16, "ho") for _ in range(2)]
        pT = [ph.ps([128, 8, 128], BF16) for _ in range(2)]
        v32 = lambda ap: ap.rearrange("p (h q) -> p h q", h=32)
        v8 = lambda ap: ap.rearrange("p (g q) -> p g q", g=8)
        for it in range(NTL // 128):
            b = it % 2
            tok = it * 128
            zrow = tok if tok < T else T + 512 + (tok - T)
            P.dma("sp", yf[b][:], YF[tok:tok + 128, :], writes=[("yf", b)])
            P.dma("sp", yb[b][:], YB[tok:tok + 128, :], writes=[("yb", b)])
            P.dma("sp", xs[b][:], XS[tok:tok + 128, :], writes=[("xs", b)])
            P.dma("sp", z[b][:], Z[zrow:zrow + 128, :], writes=[("z", b)])
            P.op("pool", lambda e, b=b: e.tensor_tensor(yf[b][:], yf[b][:], yb[b][:], ALU.add), reads=[("yf", b), ("yb", b)], writes=[("yf", b)])
            P.op("pool", lambda e, b=b: e.tensor_tensor(v32(t1[b][:]), v32(xs[b][:]), dB[:].unsqueeze(2).to_broadcast([128, 32, 64]), ALU.mult),
                 reads=[("xs", b), "dB"], writes=[("t1", b)])
            P.op("dve", lambda e, b=b: e.tensor_tensor(yf[b][:], yf[b][:], t1[b][:], ALU.add), reads=[("yf", b), ("t1", b)], writes=[("yf", b)])
            P.op("act", lambda e, b=b: e.activation(z[b][:], z[b][:], AF.Silu), reads=[("z", b)], writes=[("z", b)])
            P.op("dve", lambda e, b=b: e.tensor_tensor(yf[b][:], yf[b][:], z[b][:], ALU.mult), reads=[("yf", b), ("z", b)], writes=[("yf", b)])
            for g in range(8):
                P.op("act", lambda e, b=b, g=g: e.activation(junk[g % 2][:], yf[b][:, g * 256:(g + 1) * 256], AF.Square, accum_out=ss[b][:, g:g + 1]),
                     reads=[("yf", b)], writes=[("junk", g % 2), ("ss", b, g)])
            P.op("act", lambda e, b=b: e.activation(ss[b][:, 8:16], ss[b][:, 0:8], AF.Sqrt, scale=1.0 / 256, bias=EPS),
                 reads=[("ss", b, g) for g in range(8)], writes=[("sd", b)])
            P.op("dve", lambda e, b=b: e.reciprocal(ss[b][:, 8:16], ss[b][:, 8:16]), reads=[("sd", b)], writes=[("sd", b)])
            P.op("dve", lambda e, b=b: e.tensor_tensor(v8(t1[b][:]), v8(yf[b][:]), ss[b][:, 8:16].unsqueeze(2).to_broadcast([128, 8, 256]), ALU.mult),
                 reads=[("yf", b), ("sd", b)], writes=[("t1", b)])
            P.op("pool", lambda e, b=b: e.tensor_tensor(yn[b][:], t1[b][:], gB[:], ALU.mult), reads=[("t1", b), "gB"], writes=[("yn", b)])
            for half in range(2):
                for j in range(8):
                    kc = half * 8 + j
                    P.op("pe", lambda e, b=b, kc=kc, half=half, j=j: e.transpose(pT[half][:, j, :], yn[b][:, kc * 128:(kc + 1) * 128], ident[:]),
                         reads=[("yn", b), "ident"], writes=[("pT", half)])
                if half == 0:
                    P.op("act", lambda e, b=b: e.copy(ho[b][:, 0:8, :], pT[0][:]), reads=[("pT", 0)], writes=[("ho", b, 0)])
                else:
                    P.op("dve", lambda e, b=b: e.tensor_copy(ho[b][:, 8:16, :], pT[1][:]), reads=[("pT", 1)], writes=[("ho", b, 1)])
            P.dma("act", YT[2048:4096, tok:tok + 128].rearrange("(c p) n -> p c n", p=128), ho[b][:],
                  reads=[("ho", b, 0), ("ho", b, 1)], writes=["YT"])


def phase_na(nc, T, QT, KT, V, GT, biasD, YT, ctx_queries, tag):
    NTA = T + 768
    ROWS = T // 64
    NP = ROWS // 2
    SC = 128 ** -0.5
    with Phase(nc, "na" + tag) as ph:
        P = ph.P
        onesb = ph.sb([128, 128], BF16, "ones")
        P.op("pool", lambda e: e.memset(onesb[:], 1.0), writes=["ones"])
        kT = [ph.sb([128, NTA], BF16, "kT") for _ in range(2)]
        qT = [ph.sb([128, NTA], BF16, "qT") for _ in range(2)]
        gT = [ph.sb([128, NTA], F32, "gT") for _ in range(2)]
        vt = [ph.sb([128, NTA // 128, 128], BF16, "v") for _ in range(2)]
        bia = [ph.sb([128, 25, 128], F32, "bia") for _ in range(2)]
        yh = [ph.sb([128, T + CTX], BF16, "yh") for _ in range(2)]
        s_sb = [ph.sb([128, 640], F32, "s") for _ in range(2)]
        pl = [ph.sb([128, 896], BF16, "pl") for _ in range(2)]
        rd = [ph.sb([128, 128], F32, "rd") for _ in range(2)]
        o1 = [ph.sb([128, 128], F32, "o1") for _ in range(2)]
        sg = [ph.sb([128, 128], F32, "sg") for _ in range(2)]
        ps_a = [ph.ps() for _ in range(2)]
        ps_b = [ph.ps() for _ in range(2)]
        ps_o = [ph.ps() for _ in range(2)]
        ps_d = [ph.ps() for _ in range(2)]

        def tokoff(p):
            if 2 * p < 4:
                return T + 128 * p
            if 2 * p < 4 + ROWS:
                return 128 * p - 256
            return T + 256 + 128 * (p - 2 - NP)

        cx0 = T + 512
        ui = 0
        for h in range(16):
            hb = h % 2
            P.dma("sp", kT[hb][:], KT[h * 128:(h + 1) * 128, :], writes=[("kT", hb)])
            P.dma("sp", qT[hb][:], QT[h * 128:(h + 1) * 128, :], writes=[("qT", hb)])
            P.dma("sp", gT[hb][:], GT[h * 128:(h + 1) * 128, :], writes=[("gT", hb)])
            P.dma("sp", vt[hb][:], V[:, h * 128:(h + 1) * 128].rearrange("(t p) c -> p t c", p=128), writes=[("v", hb)])
            P.dma("sp", bia[hb][:], biasD[h], writes=[("bia", hb)])
            units = []
            for lp in range(NP):
                var = 1 if lp == 0 else 2 if lp == 1 else 3 if lp == NP - 2 else 4 if lp == NP - 1 else 0
                keys = [tokoff(lp + t) for t in range(5)] + [cx0, cx0 + 128]
                units.append((128 * lp, 128 * lp, keys, var, 5))
            if ctx_queries:
                for j in range(2):
                    units.append((cx0 + 128 * j, T + 128 * j, [cx0, cx0 + 128], None, 0))
            for (q0, y0, keys, var, nlat) in units:
                u = ui % 2
                ui += 1
                nk = len(keys)
                rdk = [("kT", hb), ("qT", hb)]
                for t, k0 in enumerate(keys):
                    dst = ps_a[u][:, t * 128:(t + 1) * 128] if t < 4 else ps_b[u][:, (t - 4) * 128:(t - 3) * 128]
                    P.op("pe", lambda e, dst=dst, k0=k0, q0=q0, hb=hb: e.matmul(dst, kT[hb][:, k0:k0 + 128], qT[hb][:, q0:q0 + 128], start=True, stop=True),
                         reads=rdk, writes=[("ps_a", u) if t < 4 else ("ps_b", u)])
                if nlat:
                    P.op("dve", lambda e, u=u, hb=hb, var=var: e.scalar_tensor_tensor(
                        s_sb[u][:, 0:512], ps_a[u][:], SC, bia[hb][:, var * 5:var * 5 + 4, :].rearrange("p t q -> p (t q)"), ALU.mult, ALU.add),
                        reads=[("ps_a", u), ("bia", hb)], writes=[("s", u, 0)])
                    P.op("dve", lambda e, u=u, hb=hb, var=var: e.scalar_tensor_tensor(
                        s_sb[u][:, 512:640], ps_b[u][:, 0:128], SC, bia[hb][:, var * 5 + 4, :], ALU.mult, ALU.add),
                        reads=[("ps_b", u), ("bia", hb)], writes=[("s", u, 1)])
                    P.op("act", lambda e, u=u: e.activation(pl[u][:, 0:640], s_sb[u][:], AF.Exp),
                         reads=[("s", u, 0), ("s", u, 1)], writes=[("pl", u, 0)])
                    P.op("act", lambda e, u=u: e.activation(pl[u][:, 640:896], ps_b[u][:, 128:384], AF.Exp, scale=SC),
                         reads=[("ps_b", u)], writes=[("pl", u, 1)])
                else:
                    P.op("act", lambda e, u=u: e.activation(pl[u][:, 0:256], ps_a[u][:, 0:256], AF.Exp, scale=SC),
                         reads=[("ps_a", u)], writes=[("pl", u, 0), ("pl", u, 1)])
                rp = [("pl", u, 0), ("pl", u, 1)]
                for t, k0 in enumerate(keys):
                    P.op("pe", lambda e, u=u, t=t, k0=k0, hb=hb, nk=nk: e.matmul(ps_o[u][:, 0:128], vt[hb][:, k0 // 128, :], pl[u][:, t * 128:(t + 1) * 128],
                                                                             start=(t == 0), stop=(t == nk - 1)),
                         reads=rp + [("v", hb)], writes=[("ps_o", u)])
                for t, k0 in enumerate(keys):
                    P.op("pe", lambda e, u=u, t=t, nk=nk: e.matmul(ps_d[u][:, 0:128], onesb[:], pl[u][:, t * 128:(t + 1) * 128],
                                                                  start=(t == 0), stop=(t == nk - 1)),
                         reads=rp + ["ones"], writes=[("ps_d", u)])
                P.op("dve", lambda e, u=u: e.reciprocal(rd[u][:], ps_d[u][:, 0:128]), reads=[("ps_d", u)], writes=[("rd", u)])
                P.op("dve", lambda e, u=u: e.tensor_tensor(o1[u][:], ps_o[u][:, 0:128], rd[u][:], ALU.mult),
                     reads=[("ps_o", u), ("rd", u)], writes=[("o1", u)])
                P.op("act", lambda e, u=u, hb=hb, q0=q0: e.activation(sg[u][:], gT[hb][:, q0:q0 + 128], AF.Silu),
                     reads=[("gT", hb)], writes=[("sg", u)])
                P.op("pool", lambda e, u=u, hb=hb, y0=y0: e.tensor_tensor(yh[hb][:, y0:y0 + 128], o1[u][:], sg[u][:], ALU.mult),
                     reads=[("o1", u), ("sg", u)], writes=[("yh", hb, y0)])
            ncol = T + (CTX if ctx_queries else 0)
            P.dma("act", YT[h * 128:(h + 1) * 128, 0:ncol], yh[hb][:, 0:ncol],
                  reads=[("yh", hb, y0) for (_, y0, _, _, _) in units], writes=["YT"])


def _even_common(nc, T):
    I = lambda n, s, d=F32: _dt(nc, n, s, d, "ExternalInput")
    D = {}
    D["xh"] = I("xh", [T + 512, 2048]); D["ctx"] = I("ctx", [CTX, 2048])
    D["c2T"] = I("c2T", [128, 16, 2]); D["ada_w"] = I("ada_w", [2048, 6144]); D["ada_b"] = I("ada_b", [6144])
    D["norm_g"] = I("norm_g", [2048]); D["w_in"] = I("w_in", [2048, EVEN_IN])
    D["cw5T"] = I("cw5T", [128, 32, 5]); D["cb5T"] = I("cb5T", [128, 32]); D["flags"] = I("flags", [128, 2])
    D["consts"] = I("consts", [4, 128, 128]); D["a_log"] = I("a_log", [64]); D["dt_bias"] = I("dt_bias", [64])
    D["ident"] = I("ident", [128, 128], BF16)
    NTA = T + 768
    NTL = T + CTX
    D["modD"] = _dt(nc, "modD", [2, 6144]); D["HT"] = _dt(nc, "HT", [2048, NTA], BF16)
    D["PX"] = _dt(nc, "PX", [4096, NTA]); D["DTR"] = _dt(nc, "DTR", [NTA, 64])
    D["XCT"] = _dt(nc, "XCT", [4096, NTL], BF16); D["XS"] = _dt(nc, "XS", [NTL, 2048], BF16); D["BM"] = _dt(nc, "BM", [NTL, 1024], BF16)
    return D, NTA, NTL


def build_pre(T):
    nc = bass.Bass("TRN2", target_bir_lowering=False)
    D, NTA, NTL = _even_common(nc, T)
    Sseg = _dt(nc, "Sseg", [2, 128, 2048], F32, "ExternalOutput")
    Dseg = _dt(nc, "Dseg", [2, 128, 32], F32, "ExternalOutput")
    phase_mod(nc, D["c2T"], D["ada_w"], D["ada_b"], D["norm_g"], D["modD"], "p")
    phase_norm(nc, [(D["xh"], 0, T + 512), (D["ctx"], 1, CTX)], D["modD"], D["HT"], D["ident"], "p")
    fams = [dict(mode="fm", W=D["w_in"], col0=COL_XBC, ncols=4096, out=D["PX"], odt=F32, norm=None),
            dict(mode="tm", W=D["w_in"], col0=COL_DT, ncols=64, out=D["DTR"], odt=F32)]
    phase_proj(nc, D["HT"], NTA, fams, "p")
    import os
    stop = int(os.environ.get("K_STOP", "9"))
    if stop >= 1:
        phase_convsilu(nc, D["PX"], T, D["cw5T"], D["cb5T"], D["flags"], D["ident"], D["XCT"], D["XS"], D["BM"], "p")
    if stop >= 2:
        phase_ssd(nc, T, D["XS"], D["BM"], D["XCT"], D["DTR"], D["consts"], D["a_log"], D["dt_bias"], "p", "pre", Sseg=Sseg, Dseg=Dseg)
    sempool(nc).close()
    return nc


def build_even(T, update_ctx, nseg):
    nc = bass.Bass("TRN2", target_bir_lowering=False)
    D, NTA, NTL = _even_common(nc, T)
    I = lambda n, s, d=F32: _dt(nc, n, s, d, "ExternalInput")
    O = lambda n, s, d=F32: _dt(nc, n, s, d, "ExternalOutput")
    Sall = I("Sall", [nseg, 2, 128, 2048]); Dall = I("Dall", [nseg, 2, 128, 32]); mk = I("mk", [128, 2 * nseg])
    d_skip = I("d_skip", [32]); ssm_g = I("ssm_g", [2048]); qg = I("qg", [128, 1]); kg = I("kg", [128, 1])
    biasD = I("biasD", [16, 128, 25, 128]); w_out = I("w_out", [4096, 2048])
    x_out = O("x_out", [T, 2048])
    ctx_out = O("ctx_out", [CTX, 2048]) if update_ctx else None
    QT = _dt(nc, "QT", [2048, NTA], BF16); KT = _dt(nc, "KT", [2048, NTA], BF16); GT = _dt(nc, "GT", [2048, NTA])
    V = _dt(nc, "V", [NTA, 2048], BF16); Z = _dt(nc, "Z", [NTA, 2048])
    YF = _dt(nc, "YF", [NTL, 2048]); YB = _dt(nc, "YB", [NTL, 2048]); YT = _dt(nc, "YT", [4096, NTL], BF16)
    w_in = D["w_in"]
    phase_mod(nc, D["c2T"], D["ada_w"], D["ada_b"], D["norm_g"], D["modD"], "e")
    phase_norm(nc, [(D["xh"], 0, T + 512), (D["ctx"], 1, CTX)], D["modD"], D["HT"], D["ident"], "e")
    fams = [dict(mode="fm", W=w_in, col0=COL_Q, ncols=2048, out=QT, odt=BF16, norm=qg),
            dict(mode="fm", W=w_in, col0=COL_K, ncols=2048, out=KT, odt=BF16, norm=kg),
            dict(mode="fm", W=w_in, col0=COL_GATE, ncols=2048, out=GT, odt=F32, norm=None),
            dict(mode="fm", W=w_in, col0=COL_XBC, ncols=4096, out=D["PX"], odt=F32, norm=None),
            dict(mode="tm", W=w_in, col0=COL_V, ncols=2048, out=V, odt=BF16),
            dict(mode="tm", W=w_in, col0=COL_Z, ncols=2048, out=Z, odt=F32),
            dict(mode="tm", W=w_in, col0=COL_DT, ncols=64, out=D["DTR"], odt=F32)]
    phase_proj(nc, D["HT"], NTA, fams, "e")
    phase_convsilu(nc, D["PX"], T, D["cw5T"], D["cb5T"], D["flags"], D["ident"], D["XCT"], D["XS"], D["BM"], "e")
    phase_ssd(nc, T, D["XS"], D["BM"], D["XCT"], D["DTR"], D["consts"], D["a_log"], D["dt_bias"], "e", "main",
              Sall=Sall, Dall=Dall, mk=mk, YF=YF, YB=YB, nseg=nseg)
    phase_gnorm(nc, T, YF, YB, D["XS"], Z, d_skip, ssm_g, D["ident"], YT, "e")
    phase_na(nc, T, QT, KT, V, GT, biasD, YT, update_ctx, "e")
    osegs = [(D["xh"], x_out, 0, T, 0)] + ([(D["ctx"], ctx_out, 1, CTX, T)] if update_ctx else [])
    phase_outproj(nc, YT, 32, w_out, osegs, D["modD"], "e")
    sempool(nc).close()
    return nc


class Idx:
    def __init__(self, fn):
        self.fn = fn

    def __getitem__(self, k):
        return self.fn(*k) if isinstance(k, tuple) else self.fn(k)


def phase_xchg(nc, X, T, sel, dst_b, dst_a, tag):
    GA_in = _dt(nc, "GAin" + tag, [128, 4096]); GB_in = _dt(nc, "GBin" + tag, [128, 4096])
    GA = _dt(nc, "GA" + tag, [1024, 4096]); GB = _dt(nc, "GB" + tag, [1024, 4096])
    v = lambda ap: ap.rearrange("(p j) d -> p (j d)", j=2)
    with Phase(nc, "xc" + tag) as ph:
        P = ph.P
        sl = ph.sb([128, 18], F32, "sel")
        P.dma("sp", sl[:], sel, writes=["sel"])
        P.dma("sp", GA_in, v(X[T - 256:T, :]), writes=["GAin"])
        P.dma("sp", GB_in, v(X[0:256, :]), writes=["GBin"])
        P.allgather(GA, GA_in, reads=["GAin"], writes=["GA"])
        P.allgather(GB, GB_in, reads=["GBin"], writes=["GB"])
        ld = [ph.sb([128, 4096], F32, "ld") for _ in range(2)]
        acc = [ph.sb([128, 4096], F32, "acc") for _ in range(2)]
        li = 0
        for w, (G, gk, own0, c0, dst) in enumerate(((GA, "GA", 256, 0, dst_b), (GB, "GB", T - 512, 9, dst_a))):
            b = li % 2
            li += 1
            P.dma("sp", ld[b][:], v(X[own0:own0 + 256, :]), writes=[("ld", b)])
            P.op("dve", lambda e, b=b, w=w, c0=c0: e.tensor_scalar(acc[w][:], ld[b][:], sl[:, c0 + 8:c0 + 9], None, ALU.mult),
                 reads=[("ld", b), "sel"], writes=[("acc", w)])
            for r in range(8):
                b = li % 2
                li += 1
                P.dma("sp", ld[b][:], G[r * 128:(r + 1) * 128, :], reads=[gk], writes=[("ld", b)])
                P.op("dve", lambda e, b=b, w=w, c0=c0, r=r: e.scalar_tensor_tensor(acc[w][:], ld[b][:], sl[:, c0 + r:c0 + r + 1], acc[w][:], ALU.mult, ALU.add),
                     reads=[("ld", b), "sel", ("acc", w)], writes=[("acc", w)])
            P.dma("act", v(dst), acc[w][:], reads=[("acc", w)], writes=["dst"])


def phase_halo_odd(nc, H, Xh, T, tag):
    with Phase(nc, "ho" + tag) as ph:
        P = ph.P
        z = ph.sb([126, 2048], F32, "z")
        t2 = ph.sb([2, 2048], F32, "t2")
        P.op("pool", lambda e: e.memset(z[:], 0.0), writes=["z"])
        P.dma("sp", t2[:], H[255:257, :], writes=["t2"])
        P.dma("act", Xh[T:T + 2, :], t2[:], reads=["t2"], writes=["Xh"])
        P.dma("act", Xh[T + 2:T + 128, :], z[:], reads=["z"], writes=["Xh"])


def build_fused(T, depth=4):
    nc = bass.Bass("TRN2", target_bir_lowering=False)
    I = lambda n, s, d=F32: _dt(nc, n, s, d, "ExternalInput")
    NTA = T + 768
    NTL = T + CTX
    xh0 = I("xh0", [T + 512, 2048]); ctx0 = I("ctx0", [CTX, 2048]); c2T = I("c2T", [128, 16, 2])
    ident = I("ident", [128, 128], BF16); consts = I("consts", [4, 128, 128])
    flags = I("flags", [128, 2]); mk = I("mk", [128, 16]); sel = I("sel", [128, 18])
    ada_w = [I("ada_w%d" % i, [2048, 6144]) for i in range(depth)]
    ada_b = [I("ada_b%d" % i, [6144]) for i in range(depth)]
    norm_g = [I("norm_g%d" % i, [2048]) for i in range(depth)]
    out = _dt(nc, "out", [T, 2048], F32, "ExternalOutput")
    n_even, n_odd = (depth + 1) // 2, depth // 2
    E = []
    for e in range(n_even):
        E.append(dict(w_in=I("w_in%d" % e, [2048, EVEN_IN]), cw5T=I("cw5T%d" % e, [128, 32, 5]), cb5T=I("cb5T%d" % e, [128, 32]),
                      a_log=I("a_log%d" % e, [64]), dt_bias=I("dt_bias%d" % e, [64]), d_skip=I("d_skip%d" % e, [32]),
                      ssm_g=I("ssm_g%d" % e, [2048]), qg=I("qg%d" % e, [128, 1]), kg=I("kg%d" % e, [128, 1]),
                      biasD=I("biasD%d" % e, [16, 128, 25, 128]), w_out=I("w_out%d" % e, [4096, 2048])))
    O = []
    for o in range(n_odd):
        O.append(dict(w_in=I("sw_in%d" % o, [2048, 8192]), cwT=I("scwT%d" % o, [128, 16, 3]), w_out=I("sw_out%d" % o, [2048, 2048])))
    modD = _dt(nc, "modD", [2, 6144]); HT = _dt(nc, "HT", [2048, NTA], BF16)
    PX = _dt(nc, "PX", [4096, NTA]); DTR = _dt(nc, "DTR", [NTA, 64])
    XCT = _dt(nc, "XCT", [4096, NTL], BF16); XS = _dt(nc, "XS", [NTL, 2048], BF16); BM = _dt(nc, "BM", [NTL, 1024], BF16)
    QT = _dt(nc, "QT", [2048, NTA], BF16); KT = _dt(nc, "KT", [2048, NTA], BF16); GT = _dt(nc, "GT", [2048, NTA])
    V = _dt(nc, "V", [NTA, 2048], BF16); Z = _dt(nc, "Z", [NTA, 2048])
    YF = _dt(nc, "YF", [NTL, 2048]); YB = _dt(nc, "YB", [NTL, 2048]); YT = _dt(nc, "YT", [4096, NTL], BF16)
    NTO = T + 128 + CTX
    HTo = _dt(nc, "HTo", [2048, NTO], BF16); PTo = _dt(nc, "PTo", [8192, NTO]); YTo = _dt(nc, "YTo", [2048, NTO], BF16)
    H = _dt(nc, "Hh", [512, 2048])
    Xe = [xh0] + [_dt(nc, "Xe%d" % e, [T + 512, 2048]) for e in range(1, n_even)]
    Xo = [_dt(nc, "Xo%d" % o, [T + 128, 2048]) for o in range(n_odd)]
    Cx = [ctx0] + [_dt(nc, "Cx%d" % i, [CTX, 2048]) for i in range(1, depth)]
    ci = 0
    for i in range(depth):
        update_ctx = any(j % 2 == 0 for j in range(i + 1, depth))
        last = (i == depth - 1)
        tg = "L%d" % i
        if i % 2 == 0:
            e = i // 2
            W = E[e]
            xin = Xe[e]
            xout = out if last else Xo[e][0:T, :]
            ctx = Cx[ci]
            phase_mod(nc, c2T, ada_w[i], ada_b[i], norm_g[i], modD, tg)
            phase_norm(nc, [(xin, 0, T + 512), (ctx, 1, CTX)], modD, HT, ident, tg)
            fams = [dict(mode="fm", W=W["w_in"], col0=COL_XBC, ncols=4096, out=PX, odt=F32, norm=None),
                    dict(mode="tm", W=W["w_in"], col0=COL_DT, ncols=64, out=DTR, odt=F32)]
            phase_proj(nc, HT, NTA, fams, tg + "a")
            phase_convsilu(nc, PX, T, W["cw5T"], W["cb5T"], flags, ident, XCT, XS, BM, tg)
            GI = _dt(nc, "GI" + tg, [128, 4160]); GO = _dt(nc, "GO" + tg, [1024, 4160])
            phase_ssd(nc, T, XS, BM, XCT, DTR, consts, W["a_log"], W["dt_bias"], tg + "p", "pre",
                      Sseg=Idx(lambda d: GI[:, d * 2080:d * 2080 + 2048]), Dseg=Idx(lambda d: GI[:, d * 2080 + 2048:d * 2080 + 2080]))
            with Phase(nc, "ag" + tg) as ph:
                ph.P.allgather(GO, GI, writes=["GO"])
                t_ = ph.sb([128, 32], F32, "t")
                ph.P.dma("sp", t_[:], GO[0:128, 2048:2080], reads=["GO"], writes=["t"])
            oblocks = [(t, min(512, T - t)) for t in range(0, T, 512)] + [(T + 512, CTX)]
            otiles = list(range(T // 128)) + [(T + 512) // 128 + j for j in range(CTX // 128)]
            fams = [dict(mode="fm", W=W["w_in"], col0=COL_Q, ncols=2048, out=QT, odt=BF16, norm=W["qg"], blocks=oblocks),
                    dict(mode="fm", W=W["w_in"], col0=COL_K, ncols=2048, out=KT, odt=BF16, norm=W["kg"]),
                    dict(mode="fm", W=W["w_in"], col0=COL_GATE, ncols=2048, out=GT, odt=F32, norm=None, blocks=oblocks),
                    dict(mode="tm", W=W["w_in"], col0=COL_V, ncols=2048, out=V, odt=BF16),
                    dict(mode="tm", W=W["w_in"], col0=COL_Z, ncols=2048, out=Z, odt=F32, tiles=otiles)]
            phase_proj(nc, HT, NTA, fams, tg + "b")
            phase_ssd(nc, T, XS, BM, XCT, DTR, consts, W["a_log"], W["dt_bias"], tg + "m", "main",
                      Sall=Idx(lambda s, d: GO[s * 128:(s + 1) * 128, d * 2080:d * 2080 + 2048]),
                      Dall=Idx(lambda s, d: GO[s * 128:(s + 1) * 128, d * 2080 + 2048:d * 2080 + 2080]),
                      mk=mk, YF=YF, YB=YB, nseg=8)
            phase_gnorm(nc, T, YF, YB, XS, Z, W["d_skip"], W["ssm_g"], ident, YT, tg)
            phase_na(nc, T, QT, KT, V, GT, W["biasD"], YT, update_ctx, tg)
            osegs = [(xin, xout, 0, T, 0)]
            if update_ctx:
                osegs.append((ctx, Cx[ci + 1], 1, CTX, T))
            phase_outproj(nc, YT, 32, W["w_out"], osegs, modD, tg)
            if update_ctx:
                ci += 1
            if not last:
                phase_xchg(nc, Xo[e], T, sel, H[0:256, :], H[256:512, :], tg)
                phase_halo_odd(nc, H, Xo[e], T, tg)
        else:
            o = i // 2
            W = O[o]
            xin = Xo[o]
            xout = out if last else Xe[o + 1][0:T, :]
            ctx = Cx[ci]
            NT_ = T + 128 + (CTX if update_ctx else 0)
            phase_mod(nc, c2T, ada_w[i], ada_b[i], norm_g[i], modD, tg)
            segs = [(xin, 0, T + 128)] + ([(ctx, 1, CTX)] if update_ctx else [])
            phase_norm(nc, segs, modD, HTo[:, 0:NT_], ident, tg)
            phase_proj(nc, HTo[:, 0:NT_], NT_, [dict(mode="fm", W=W["w_in"], col0=0, ncols=8192, out=PTo[:, 0:NT_], odt=F32, norm=None)], tg)
            seqs = [(0, T, (T, T + 1))] + ([(T + 128, CTX, None)] if update_ctx else [])
            phase_convgate(nc, PTo[:, 0:NT_], NT_, seqs, W["cwT"], flags, YTo[:, 0:NT_], tg)
            osegs = [(xin, xout, 0, T, 0)]
            if update_ctx:
                osegs.append((ctx, Cx[ci + 1], 1, CTX, T + 128))
            phase_outproj(nc, YTo[:, 0:NT_], 16, W["w_out"], osegs, modD, tg)
            if update_ctx:
                ci += 1
            if not last:
                Xn = Xe[o + 1]
                phase_xchg(nc, Xn, T, sel, Xn[T:T + 256, :], Xn[T + 256:T + 512, :], tg)
    sempool(nc).close()
    return nc


def _bias_tables(rpb, R0, ROWS, RTOT):
    NP = ROWS // 2
    first, last = (R0 == 0), (R0 + ROWS == RTOT)
    lp_of = [min(2, NP - 3) if NP > 4 else 0, 0, 1, NP - 2, NP - 1]
    kr = np.arange(128) // 64
    kc = np.arange(128) % 64
    qr, qc = kr, kc
    out = np.empty((16, 128, 25, 128), np.float32)
    cs = np.clip(qc - 8, 0, 48)
    colok = (kc[:, None] >= cs[None, :]) & (kc[:, None] < cs[None, :] + 16)
    dc = np.clip(kc[:, None] - qc[None, :], -15, 15) + 15
    for v, lp in enumerate(lp_of):
        for t in range(5):
            e = 2 * (lp + t) + kr
            l = 2 * lp + qr
            o = e[:, None] - l[None, :]
            rowok = (o >= 0) & (o <= 7)
            loc = e - 4
            g = R0 + loc
            if first:
                g = np.where(e < 4, 4 + e, g)
            if last:
                g = np.where(loc >= ROWS, RTOT - 8 + (loc - ROWS), g)
            qrow = R0 + l
            dr = g[:, None] - qrow[None, :] + 7
            ok = rowok & colok & (dr >= 0) & (dr <= 14)
            drc = np.clip(dr, 0, 14)
            vals = rpb[:, drc, dc]
            out[:, :, v * 5 + t, :] = np.where(ok[None], vals, np.float32(NEG))
    return out


_CONSTS = None


def _consts():
    global _CONSTS
    if _CONSTS is None:
        t = np.arange(128)
        triF = (t[:, None] <= t[None, :]).astype(np.float32)
        triB = (t[:, None] >= t[None, :]).astype(np.float32)
        LsF = (t[:, None] > t[None, :]).astype(np.float32)
        LsB = (t[:, None] < t[None, :]).astype(np.float32)
        _CONSTS = np.ascontiguousarray(np.stack([triF, triB, LsF, LsB]))
    return _CONSTS


_PROGS = {}


def _prog(key, fn):
    if key not in _PROGS:
        _PROGS[key] = fn()
    return _PROGS[key]


def _pT(v, nchunk):
    k = v.shape[0]
    return np.ascontiguousarray(v.reshape(k, nchunk, 128).transpose(2, 1, 0))


def _run_model(inp, seq, nseg):
    import ml_dtypes
    B = inp["x"].shape[0]
    T = seq // nseg
    ROWS = T // GRID_W
    RTOT = seq // GRID_W
    ncores = B * nseg
    cores = [(b, s) for b in range(B) for s in range(nseg)]
    x = np.array(inp["x"], np.float32)
    ctx = np.array(inp["ctx"], np.float32)
    ident = np.eye(128, dtype=np.float32).astype(ml_dtypes.bfloat16)
    depth = inp["ada_w"].shape[0]
    for i in range(depth):
        update_ctx = any(j % 2 == 0 for j in range(i + 1, depth))
        c2T = [_pT(np.stack([inp["c"][b], inp["c_ctx"]]), 16) for b in range(B)]
        base = dict(ada_w=inp["ada_w"][i], ada_b=inp["ada_b"][i], norm_g=inp["norm_g"][i], ident=ident)
        if i % 2 == 0:
            e = i // 2
            xbc_w = inp["ssd_conv_w"][e]
            common = dict(base, w_in=inp["na_ssd_w_in"][e], cw5T=_pT(xbc_w, 32),
                          cb5T=np.ascontiguousarray(_pT(inp["ssd_conv_b"][e][None], 32)[:, :, 0]),
                          consts=_consts(), a_log=np.ascontiguousarray(inp["ssd_a_log"][e].reshape(64)),
                          dt_bias=np.ascontiguousarray(inp["ssd_dt_bias"][e].reshape(64)))
            maps = []
            for (b, s) in cores:
                R0 = s * ROWS
                own = x[b, s * T:(s + 1) * T]
                hb0 = (4 if s == 0 else R0 - 4) * GRID_W
                ha0 = (RTOT - 8 if s == nseg - 1 else R0 + ROWS) * GRID_W
                xh = np.concatenate([own, x[b, hb0:hb0 + 256], x[b, ha0:ha0 + 192], np.zeros((64, D_MODEL), np.float32)], 0)
                fl = np.zeros((128, 2), np.float32)
                fl[:, 0] = 0.0 if s == 0 else 1.0
                fl[:, 1] = 0.0 if s == nseg - 1 else 1.0
                maps.append(dict(common, xh=xh, ctx=ctx[b], c2T=c2T[b], flags=fl))
            pre = _prog(("pre", T), lambda: build_pre(T))
            r = run_bass_kernel_spmd(pre, maps, core_ids=list(range(ncores))).results
            Sall = [np.ascontiguousarray(np.stack([r[b * nseg + s]["Sseg"] for s in range(nseg)])) for b in range(B)]
            Dall = [np.ascontiguousarray(np.stack([r[b * nseg + s]["Dseg"] for s in range(nseg)])) for b in range(B)]
            for ci, (b, s) in enumerate(cores):
                mk = np.zeros((128, 2 * nseg), np.float32)
                mk[:, :s] = 1.0
                mk[:, nseg + s + 1:] = 1.0
                maps[ci].update(Sall=Sall[b], Dall=Dall[b], mk=mk, d_skip=inp["ssd_d"][e], ssm_g=inp["ssd_norm_g"][e],
                                qg=np.ascontiguousarray(inp["q_norm_g"][e][:, None]), kg=np.ascontiguousarray(inp["k_norm_g"][e][:, None]),
                                biasD=_bias_tables(inp["na_rpb"][e], s * ROWS, ROWS, RTOT), w_out=inp["na_ssd_w_out"][e])
            main = _prog(("even", T, update_ctx, nseg), lambda: build_even(T, update_ctx, nseg))
            r = run_bass_kernel_spmd(main, maps, core_ids=list(range(ncores))).results
        else:
            o = i // 2
            common = dict(base, w_in=inp["sc_w_in"][o], cwT=_pT(inp["sc_conv_w"][o], 16), w_out=inp["sc_w_out"][o])
            maps = []
            for (b, s) in cores:
                own = x[b, s * T:(s + 1) * T]
                halo = np.zeros((128, D_MODEL), np.float32)
                fl = np.zeros((128, 2), np.float32)
                if s > 0:
                    halo[0] = x[b, s * T - 1]
                    fl[:, 0] = 1.0
                if s < nseg - 1:
                    halo[1] = x[b, (s + 1) * T]
                    fl[:, 1] = 1.0
                m = dict(common, xh=np.concatenate([own, halo], 0), c2T=c2T[b], flags=fl)
                if update_ctx:
                    m["ctx"] = ctx[b]
                maps.append(m)
            prog = _prog(("odd", T, update_ctx), lambda: build_odd(T, update_ctx))
            r = run_bass_kernel_spmd(prog, maps, core_ids=list(range(ncores))).results
        xn = np.empty_like(x)
        for ci, (b, s) in enumerate(cores):
            xn[b, s * T:(s + 1) * T] = r[ci]["x_out"]
        x = xn
        if update_ctx:
            ctx = np.stack([r[b * nseg]["ctx_out"] for b in range(B)])
    return x


def _run_fused(inp, seq, nseg):
    import ml_dtypes
    B = inp["x"].shape[0]
    T = seq // nseg
    ROWS = T // GRID_W
    RTOT = seq // GRID_W
    ncores = B * nseg
    assert ncores == 8
    depth = inp["ada_w"].shape[0]
    x = np.asarray(inp["x"], np.float32)
    shared = dict(ident=np.eye(128, dtype=np.float32).astype(ml_dtypes.bfloat16), consts=_consts())
    for i in range(depth):
        shared["ada_w%d" % i] = inp["ada_w"][i]
        shared["ada_b%d" % i] = inp["ada_b"][i]
        shared["norm_g%d" % i] = inp["norm_g"][i]
    for e in range((depth + 1) // 2):
        shared.update({"w_in%d" % e: inp["na_ssd_w_in"][e], "cw5T%d" % e: _pT(inp["ssd_conv_w"][e], 32),
                       "cb5T%d" % e: np.ascontiguousarray(_pT(inp["ssd_conv_b"][e][None], 32)[:, :, 0]),
                       "a_log%d" % e: np.ascontiguousarray(inp["ssd_a_log"][e].reshape(64)),
                       "dt_bias%d" % e: np.ascontiguousarray(inp["ssd_dt_bias"][e].reshape(64)),
                       "d_skip%d" % e: inp["ssd_d"][e], "ssm_g%d" % e: inp["ssd_norm_g"][e],
                       "qg%d" % e: np.ascontiguousarray(inp["q_norm_g"][e][:, None]),
                       "kg%d" % e: np.ascontiguousarray(inp["k_norm_g"][e][:, None]), "w_out%d" % e: inp["na_ssd_w_out"][e]})
    for o in range(depth // 2):
        shared.update({"sw_in%d" % o: inp["sc_w_in"][o], "scwT%d" % o: _pT(inp["sc_conv_w"][o], 16), "sw_out%d" % o: inp["sc_w_out"][o]})
    maps = []
    for b in range(B):
        c2T = _pT(np.stack([inp["c"][b], inp["c_ctx"]]), 16)
        for s in range(nseg):
            rank = b * nseg + s
            R0 = s * ROWS
            own = x[b, s * T:(s + 1) * T]
            hb0 = (4 if s == 0 else R0 - 4) * GRID_W
            ha0 = (RTOT - 8 if s == nseg - 1 else R0 + ROWS) * GRID_W
            xh = np.concatenate([own, x[b, hb0:hb0 + 256], x[b, ha0:ha0 + 192], np.zeros((64, D_MODEL), np.float32)], 0)
            fl = np.zeros((128, 2), np.float32)
            fl[:, 0] = 0.0 if s == 0 else 1.0
            fl[:, 1] = 0.0 if s == nseg - 1 else 1.0
            mk = np.zeros((128, 16), np.float32)
            for r in range(8):
                if r // nseg == b and r < rank:
                    mk[:, r] = 1.0
                if r // nseg == b and r > rank:
                    mk[:, 8 + r] = 1.0
            sel = np.zeros((128, 18), np.float32)
            if s == 0:
                sel[:, 8] = 1.0
            else:
                sel[:, rank - 1] = 1.0
            if s == nseg - 1:
                sel[:, 17] = 1.0
            else:
                sel[:, 9 + rank + 1] = 1.0
            m = dict(shared, xh0=xh, ctx0=np.asarray(inp["ctx"][b], np.float32), c2T=c2T, flags=fl, mk=mk, sel=sel)
            for e in range((depth + 1) // 2):
                m["biasD%d" % e] = _bias_tables(inp["na_rpb"][e], R0, ROWS, RTOT)
            maps.append(m)
    prog = _prog(("fused", T, depth), lambda: build_fused(T, depth))
    r = run_bass_kernel_spmd(prog, maps, core_ids=list(range(ncores))).results
    out = np.empty_like(x)
    for b in range(B):
        for s in range(nseg):
            out[b, s * T:(s + 1) * T] = r[b * nseg + s]["out"]
    return out


def kernel(**inputs):
    inp = {k: np.asarray(v) for k, v in inputs.items()}
    return _run_fused(inp, inp["x"].shape[1], NSEG)
```
